# Optimizing a Trainium2 kernel written in Bass

```python
import math
import jax, jax.numpy as jnp
from jax import lax
import numpy as np

D_MODEL = 4096
BATCH = 4
SEQ = 4096
DEPTH = 2
DEC_BATCH = 8
DEC_SEQ = 2048
PAST_LEN = 128

HEAD_DIM = 128
GRID_W = 64
NA_HEADS = 12
NA_WIN_R = 8
NA_WIN_C = 16
DIFF_HEADS = 4
GQA_Q_HEADS = 12
GQA_KV_HEADS = 4
D_FF = 4 * D_MODEL
ROPE_THETA = 10000.0
Q_BLOCK = 128
NORM_EPS = 1e-6

W_A = NA_HEADS * HEAD_DIM
W_B_QK = 2 * DIFF_HEADS * HEAD_DIM
W_B_V = DIFF_HEADS * 2 * HEAD_DIM
W_C_Q = GQA_Q_HEADS * HEAD_DIM
W_C_KV = GQA_KV_HEADS * HEAD_DIM
SPLIT_SIZES = (W_A, W_A, W_A, W_B_QK, W_B_QK, W_B_V, W_C_Q, W_C_KV, W_C_KV, D_MODEL, D_MODEL, D_MODEL)
N_IN = 3 * W_A + 2 * W_B_QK + W_B_V + W_C_Q + 2 * W_C_KV + 3 * D_MODEL

kernel_name = 'hybrid_natten_diff_gqa_encoder'


def rms_norm(x, g):
    xf = x.astype(jnp.float32)
    y = xf * lax.rsqrt(jnp.mean(xf * xf, axis=-1, keepdims=True) + NORM_EPS)
    return (y * g.astype(jnp.float32)).astype(x.dtype)


def rope(x, pos):
    dr = x.shape[-1]
    inv = ROPE_THETA ** (-jnp.arange(0, dr, 2, dtype=jnp.float32) / dr)
    ang = pos.astype(jnp.float32)[:, None] * inv[None, :]
    cos = jnp.cos(ang)[None, :, None, :]
    sin = jnp.sin(ang)[None, :, None, :]
    x1, x2 = jnp.split(x.astype(jnp.float32), 2, axis=-1)
    return jnp.concatenate([x1 * cos - x2 * sin, x2 * cos + x1 * sin], axis=-1).astype(x.dtype)


def axial_rope(x, pos):
    half = x.shape[-1] // 2
    return jnp.concatenate([rope(x[..., :half], pos // GRID_W), rope(x[..., half:], pos % GRID_W)], axis=-1)


def neighbourhood_attention(q, k, v, rpb):
    b, s, h, dh = q.shape
    rows = s // GRID_W
    kr = min(NA_WIN_R, rows)
    qg = q.reshape(b, rows, GRID_W, h, dh)
    kg = k.reshape(b, rows, GRID_W, h, dh)
    vg = v.reshape(b, rows, GRID_W, h, dh)
    col = jnp.arange(GRID_W)
    col_start = jnp.clip(col - NA_WIN_C // 2, 0, GRID_W - NA_WIN_C)
    col_mask = (col[None, :] >= col_start[:, None]) & (col[None, :] < col_start[:, None] + NA_WIN_C)
    dc = jnp.clip(col[None, :] - col[:, None], -(NA_WIN_C - 1), NA_WIN_C - 1) + NA_WIN_C - 1
    rpb_c = rpb[:, :, dc]
    scale = dh ** -0.5

    def row_block(r):
        rs = jnp.clip(r - kr // 2, 0, rows - kr)
        q_r = lax.dynamic_index_in_dim(qg, r, axis=1, keepdims=False)
        k_r = lax.dynamic_slice_in_dim(kg, rs, kr, axis=1)
        v_r = lax.dynamic_slice_in_dim(vg, rs, kr, axis=1)
        dr = rs + jnp.arange(kr) - r + NA_WIN_R - 1
        bias = rpb_c[:, dr].transpose(0, 2, 1, 3)
        sc = jnp.einsum('bqhd,bkwhd->bhqkw', q_r, k_r, preferred_element_type=jnp.float32) * scale
        sc = sc + bias[None].astype(jnp.float32)
        sc = jnp.where(col_mask[None, None, :, None, :], sc, -jnp.inf)
        p = jax.nn.softmax(sc.reshape(b, h, GRID_W, kr * GRID_W), axis=-1).reshape(b, h, GRID_W, kr, GRID_W)
        return jnp.einsum('bhqkw,bkwhd->bqhd', p.astype(v.dtype), v_r)

    out = lax.map(row_block, jnp.arange(rows))
    return out.transpose(1, 0, 2, 3, 4).reshape(b, s, h * dh)


def diff_attention(q, k, v, lam, lam_init, sub_gain):
    b, s, h2, dh = q.shape
    hd = h2 // 2
    nblk = s // Q_BLOCK
    qb = q.reshape(b, nblk, Q_BLOCK, h2, dh).transpose(1, 0, 2, 3, 4)
    scale = dh ** -0.5

    def block(q_blk):
        sc = jnp.einsum('bqhd,bkhd->bhqk', q_blk, k, preferred_element_type=jnp.float32) * scale
        p = jax.nn.softmax(sc, axis=-1).reshape(b, hd, 2, Q_BLOCK, s)
        p = p[:, :, 0] - lam * p[:, :, 1]
        return jnp.einsum('bhqk,bkhe->bqhe', p.astype(v.dtype), v)

    o = lax.map(block, qb).transpose(1, 0, 2, 3, 4).reshape(b, s, hd, 2 * dh)
    o = rms_norm(o, sub_gain) * (1.0 - lam_init)
    return o.reshape(b, s, hd * 2 * dh)


def gqa_attention(q, k, v):
    b, s, hq, dh = q.shape
    hkv = k.shape[2]
    g = hq // hkv
    nblk = s // Q_BLOCK
    qb = q.reshape(b, nblk, Q_BLOCK, hkv, g, dh).transpose(1, 0, 2, 3, 4, 5)
    scale = dh ** -0.5

    def block(q_blk):
        sc = jnp.einsum('bqngd,btnd->bngqt', q_blk, k, preferred_element_type=jnp.float32) * scale
        p = jax.nn.softmax(sc, axis=-1)
        return jnp.einsum('bngqt,btnd->bqngd', p.astype(v.dtype), v)

    o = lax.map(block, qb).transpose(1, 0, 2, 3, 4, 5)
    return o.reshape(b, s, hq * dh)


def encoder_layer(x, lam_init, norm_mix, w_in, qn_a, kn_a, rpb, qn_b, kn_b, lam_q1, lam_k1, lam_q2, lam_k2,
                  subln_b, qn_c, kn_c, w_oa, w_ob, w_oc, w_out, norm_mlp, w_up, w_down):
    b, s, _ = x.shape
    h = rms_norm(x, norm_mix)
    proj = jnp.einsum('bsd,dn->bsn', h, w_in)
    points = []
    acc = 0
    for n in SPLIT_SIZES[:-1]:
        acc += n
        points.append(acc)
    qa, ka, va, qb, kb, vb, qc, kc, vc, ga, gb, gc = jnp.split(proj, points, axis=-1)
    pos = jnp.arange(s, dtype=jnp.int32)

    qa = rms_norm(qa.reshape(b, s, NA_HEADS, HEAD_DIM), qn_a)
    ka = rms_norm(ka.reshape(b, s, NA_HEADS, HEAD_DIM), kn_a)
    o_a = neighbourhood_attention(qa, ka, va.reshape(b, s, NA_HEADS, HEAD_DIM), rpb)

    qb = rope(rms_norm(qb.reshape(b, s, 2 * DIFF_HEADS, HEAD_DIM), qn_b), pos)
    kb = rope(rms_norm(kb.reshape(b, s, 2 * DIFF_HEADS, HEAD_DIM), kn_b), pos)
    f32 = jnp.float32
    lam = (jnp.exp(jnp.sum(lam_q1.astype(f32) * lam_k1.astype(f32)))
           - jnp.exp(jnp.sum(lam_q2.astype(f32) * lam_k2.astype(f32))) + lam_init)
    o_b = diff_attention(qb, kb, vb.reshape(b, s, DIFF_HEADS, 2 * HEAD_DIM), lam, lam_init, subln_b)

    qc = axial_rope(rms_norm(qc.reshape(b, s, GQA_Q_HEADS, HEAD_DIM), qn_c), pos)
    kc = axial_rope(rms_norm(kc.reshape(b, s, GQA_KV_HEADS, HEAD_DIM), kn_c), pos)
    o_c = gqa_attention(qc, kc, vc.reshape(b, s, GQA_KV_HEADS, HEAD_DIM))

    merged = (jax.nn.sigmoid(ga) * (o_a @ w_oa)
              + jax.nn.sigmoid(gb) * (o_b @ w_ob)
              + jax.nn.sigmoid(gc) * (o_c @ w_oc))
    x = x + merged @ w_out

    h = rms_norm(x, norm_mlp)
    u = jnp.square(jax.nn.relu(h @ w_up))
    return x + u @ w_down


def setup_inputs(seed: int = 0) -> dict:
    key = jax.random.key(seed)
    ks = jax.random.split(key, 24)
    f32 = jnp.float32

    def nrm(k, shape, scale):
        return jax.random.normal(k, shape, f32) * scale

    def gain(k, shape):
        return 1.0 + 0.05 * jax.random.normal(k, shape, f32)

    return {
        'x_prompt': nrm(ks[0], (BATCH, SEQ, D_MODEL), 1.0),
        'x_sample': nrm(ks[1], (DEC_BATCH, DEC_SEQ, D_MODEL), 1.0),
        'norm_mix': gain(ks[2], (DEPTH, D_MODEL)),
        'w_in': nrm(ks[3], (DEPTH, D_MODEL, N_IN), D_MODEL ** -0.5),
        'qn_a': gain(ks[4], (DEPTH, HEAD_DIM)),
        'kn_a': gain(ks[5], (DEPTH, HEAD_DIM)),
        'rpb': nrm(ks[6], (DEPTH, NA_HEADS, 2 * NA_WIN_R - 1, 2 * NA_WIN_C - 1), 0.1),
        'qn_b': gain(ks[7], (DEPTH, HEAD_DIM)),
        'kn_b': gain(ks[8], (DEPTH, HEAD_DIM)),
        'lam_q1': nrm(ks[9], (DEPTH, HEAD_DIM), 0.1),
        'lam_k1': nrm(ks[10], (DEPTH, HEAD_DIM), 0.1),
        'lam_q2': nrm(ks[11], (DEPTH, HEAD_DIM), 0.1),
        'lam_k2': nrm(ks[12], (DEPTH, HEAD_DIM), 0.1),
        'subln_b': gain(ks[13], (DEPTH, 2 * HEAD_DIM)),
        'qn_c': gain(ks[14], (DEPTH, HEAD_DIM)),
        'kn_c': gain(ks[15], (DEPTH, HEAD_DIM)),
        'w_oa': nrm(ks[16], (DEPTH, W_A, D_MODEL), W_A ** -0.5),
        'w_ob': nrm(ks[17], (DEPTH, W_B_V, D_MODEL), W_B_V ** -0.5),
        'w_oc': nrm(ks[18], (DEPTH, W_C_Q, D_MODEL), W_C_Q ** -0.5),
        'w_out': nrm(ks[19], (DEPTH, D_MODEL, D_MODEL), D_MODEL ** -0.5),
        'norm_mlp': gain(ks[20], (DEPTH, D_MODEL)),
        'w_up': nrm(ks[21], (DEPTH, D_MODEL, D_FF), D_MODEL ** -0.5),
        'w_down': nrm(ks[22], (DEPTH, D_FF, D_MODEL), D_FF ** -0.5),
    }


def reference(x_prompt, x_sample, norm_mix, w_in, qn_a, kn_a, rpb, qn_b, kn_b, lam_q1, lam_k1, lam_q2, lam_k2,
              subln_b, qn_c, kn_c, w_oa, w_ob, w_oc, w_out, norm_mlp, w_up, w_down):
    def trunk(x):
        for l in range(DEPTH):
            lam_init = 0.8 - 0.6 * math.exp(-0.3 * l)
            x = encoder_layer(x, lam_init, norm_mix[l], w_in[l], qn_a[l], kn_a[l], rpb[l], qn_b[l], kn_b[l],
                              lam_q1[l], lam_k1[l], lam_q2[l], lam_k2[l], subln_b[l], qn_c[l], kn_c[l],
                              w_oa[l], w_ob[l], w_oc[l], w_out[l], norm_mlp[l], w_up[l], w_down[l])
        return x

    y_prompt = trunk(x_prompt)
    y_sample = trunk(x_sample)
    return (y_prompt, y_sample)
```

```python
import contextlib
import math
import numpy as np
import ml_dtypes
import concourse.bass as bass
import concourse.mybir as mybir
from concourse.bass_utils import run_bass_kernel_spmd

F32 = mybir.dt.float32
BF16 = mybir.dt.bfloat16
AF = mybir.ActivationFunctionType
ALU = mybir.AluOpType
AX = mybir.AxisListType
ENGS = ("sync", "act", "dve", "pool", "pe")
NEG = -30000.0
EPS = 1e-6


class _Op:
    __slots__ = ("eng", "emit", "deps", "dma_key", "needed", "val", "semk", "i", "nobar")


class Sched:
    def __init__(self):
        self.ops = []
        self.last_writer = {}
        self.readers = {}
        self.last_dma = {}
        self.last_eng = {}

    def add(self, eng, emit, reads=(), writes=(), dma_key=None, nobar=False, extra=()):
        op = _Op()
        op.eng = eng
        op.emit = emit
        op.dma_key = dma_key
        op.needed = False
        op.nobar = nobar
        op.i = len(self.ops)
        deps = list(extra)
        for r in reads:
            w = self.last_writer.get(r)
            if w is not None:
                deps.append(w)
        for w_ in writes:
            w = self.last_writer.get(w_)
            if w is not None:
                deps.append(w)
            rd = self.readers.get(w_)
            if rd:
                deps.extend(rd.values())
        if dma_key is not None:
            p = self.last_dma.get(dma_key)
            if p is not None:
                deps.append(p)
            self.last_dma[dma_key] = op
        op.deps = deps
        for w_ in writes:
            self.last_writer[w_] = op
            self.readers[w_] = {}
        for r in reads:
            d = self.readers.setdefault(r, {})
            if dma_key is not None:
                d[("dma", op.i)] = op
            else:
                d[eng] = op
        if emit is not None and dma_key is None:
            self.last_eng[eng] = op
        self.ops.append(op)
        return op

    def barrier(self):
        deps = [o for o in self.last_eng.values()]
        deps += [o for o in self.last_dma.values() if not o.nobar]
        for e in ENGS:
            self.add(e, None, extra=deps)

    def finalize(self, nc, stack, epoch=30000):
        for op in self.ops:
            keep = []
            seen = set()
            for d in op.deps:
                if d.i in seen:
                    continue
                seen.add(d.i)
                if d.eng == "pe" and op.eng == "pe" and d.dma_key is None and op.dma_key is None and op.emit is not None:
                    continue
                keep.append(d)
                d.needed = True
            op.deps = keep
        sems = {}

        def getsem(k):
            if k not in sems:
                sems[k] = stack.enter_context(nc.semaphore("s%d" % len(sems)))
            return sems[k]

        cnt = {e: 0 for e in ENGS}
        ep = {e: 0 for e in ENGS}
        dcnt = {}
        for op in self.ops:
            if op.dma_key is not None:
                k = ("dma", op.dma_key)
                dcnt[k] = dcnt.get(k, 0) + 16
                op.semk = k
                op.val = dcnt[k]
            elif op.needed:
                if cnt[op.eng] >= epoch:
                    ep[op.eng] += 1
                    cnt[op.eng] = 0
                cnt[op.eng] += 1
                op.semk = ("eng", op.eng, ep[op.eng])
                op.val = cnt[op.eng]
            else:
                op.semk = None
                op.val = 0
        waited = {e: {} for e in ENGS}
        streams = {e: [] for e in ENGS}
        for op in self.ops:
            ws = {}
            for d in op.deps:
                if waited[op.eng].get(d.semk, 0) >= d.val:
                    continue
                if ws.get(d.semk, 0) < d.val:
                    ws[d.semk] = d.val
            for k, v in ws.items():
                waited[op.eng][k] = v
            inc = None
            if op.semk is not None:
                inc = (getsem(op.semk), 16 if op.dma_key is not None else 1)
            streams[op.eng].append(([(getsem(k), v) for k, v in ws.items()], op.emit, inc))
        self.streams = streams
        fin = []
        for k, v in list(dcnt.items()):
            fin.append((getsem(k), v))
        for e in ENGS:
            for epi in range(ep[e] + 1):
                k = ("eng", e, epi)
                if k in sems:
                    fin.append((sems[k], epoch if epi < ep[e] else cnt[e]))
        self.fin = fin
        self.nsems = len(sems)

    def emit_all(self, nc):
        with nc.Block() as block:
            def run(e, name):
                for waits, emit, inc in self.streams[name]:
                    for s, v in waits:
                        e.wait_ge(s, v)
                    if emit is None:
                        continue
                    ins = emit(e)
                    if inc is not None:
                        ins.then_inc(inc[0], inc[1])
                if name == "sync":
                    for s, v in self.fin:
                        e.wait_ge(s, v)

            @block.sync
            def _(e):
                run(e, "sync")

            @block.scalar
            def _(e):
                run(e, "act")

            @block.vector
            def _(e):
                run(e, "dve")

            @block.gpsimd
            def _(e):
                run(e, "pool")

            @block.tensor
            def _(e):
                run(e, "pe")


def mkcfg(D=4096, NT=4096, HA=12, HB=4, HQ=12, HKV=4, FF=16384, L=2):
    c = dict(D=D, NT=NT, HA=HA, HB=HB, HQ=HQ, HKV=HKV, FF=FF, L=L)
    c["T"] = 512
    c["KC"] = D // 128
    c["segs"] = [("qa", HA * 128), ("ka", HA * 128), ("va", HA * 128), ("qb", 2 * HB * 128), ("kb", 2 * HB * 128),
                 ("vb", HB * 256), ("qc", HQ * 128), ("kc", HKV * 128), ("vc", HKV * 128), ("ga", D), ("gb", D), ("gc", D)]
    c["NIN"] = sum(w for _, w in c["segs"])
    c["NQKH"] = 2 * HA + 4 * HB + HQ + HKV
    c["VW"] = HA * 128 + HB * 256 + HKV * 128
    return c


def windows(cfg):
    rows = cfg["NT"] // 64
    rs_ = rows // 2
    out = []
    ncol = 0
    for r in range(rows):
        sp = min(max(r - 4, 0), rows - 8)
        base = 0 if r < rs_ else rs_
        ss_ = base + min(max(r - base - 4, 0), rs_ - 8)
        lo, hi = min(sp, ss_), max(sp, ss_) + 8
        if (hi - lo) % 2:
            if hi < rows:
                hi += 1
            else:
                lo -= 1
        nt = (hi - lo) // 2
        special = not (sp == ss_ and hi - lo == 8)
        dr0 = lo - r + 7
        out.append(dict(r=r, lo=lo, nt=nt, special=special, dr0=dr0, wp=(sp, sp + 8), ws=(ss_, ss_ + 8), col=ncol))
        for t in range(nt):
            for kr in (lo + 2 * t, lo + 2 * t + 1):
                valid = (sp <= kr < sp + 8) or (ss_ <= kr < ss_ + 8)
                if valid:
                    assert 0 <= dr0 + 2 * t <= 13 and -7 <= kr - r <= 7, (r, kr)
        assert nt * 64 <= 512
        if special:
            ncol += nt
    return out, max(ncol, 1)


def build(cfg):
    D, NT, HA, HB, HQ, HKV, FF, L = (cfg[k] for k in ("D", "NT", "HA", "HB", "HQ", "HKV", "FF", "L"))
    T, KC, NIN, NQKH, VW = cfg["T"], cfg["KC"], cfg["NIN"], cfg["NQKH"], cfg["VW"]
    NTT, NS, NKT, NQB = NT // T, T // 128, NT // 128, NT // 512
    G = HQ // HKV
    NSUB = NT // 128
    wins, NRM = windows(cfg)
    rows = NT // 64
    KSH = (FF // 2) // D
    NUPB = (FF // 2) // 512
    assert KSH * D * 2 == FF

    nc = bass.Bass("TRN2", target_bir_lowering=False)

    def inp(name, shape, dt=F32):
        return nc.dram_tensor(name, list(shape), dt, kind="ExternalInput").ap()

    def scr(name, shape, dt):
        return nc.dram_tensor(name, list(shape), dt, kind="Internal").ap()

    x_in = inp("x", [NT, D])
    w_in = inp("w_in", [L, D, NIN])
    w_oa = inp("w_oa", [L, HA * 128, D])
    w_ob = inp("w_ob", [L, HB * 256, D])
    w_oc = inp("w_oc", [L, HQ * 128, D])
    w_out = inp("w_out", [L, D, D])
    w_up = inp("w_up", [L, D, FF])
    w_down = inp("w_down", [L, FF, D])
    g1 = inp("g1", [L, 128, D])
    g2 = inp("g2", [L, 128, D])
    gains_in = inp("gains", [L, 128, 6 * 128])
    lamv_in = inp("lamv", [L, 128, 4 * 128])
    subln_in = inp("subln", [L, 128, 2])
    tab_in = inp("tab", [L, 128, HA * 14 * 64])
    colmask_in = inp("colmask", [128, 64])
    rowmask_in = inp("rowmask", [128, NRM])
    seqmask_in = inp("seqmask", [128, NQB * NKT])
    rope_in = inp("rope", [NT, 4 * 128])
    ident_in = inp("ident", [128, 128], BF16)
    ones_in = inp("ones", [128, 128], BF16)
    y_out = nc.dram_tensor("y", [NT, D], F32, kind="ExternalOutput").ap()

    wb_in = [scr("wb_in%d" % l, [D, NIN], BF16) for l in range(L)]
    wb_o = [scr("wb_o%d" % l, [D, D], BF16) for l in range(L)]
    wb_out = [scr("wb_out%d" % l, [D, D], BF16) for l in range(L)]
    wb_up = [scr("wb_up%d" % l, [D, FF], BF16) for l in range(L)]
    wb_down = [scr("wb_down%d" % l, [FF, D], BF16) for l in range(L)]
    QT = scr("QT", [NQKH, 128, NT], BF16)
    VS = scr("VS", [NT, VW], BF16)
    GS = scr("GS", [3 * D, NT], BF16)
    OT = scr("OT", [D, NT], BF16)
    XB = scr("XB", [NT, D], F32)

    S = Sched()
    st = contextlib.ExitStack()
    with st:
        ARENA_B = 207 * 1024
        arena = st.enter_context(nc.sbuf_tensor("arena", [128, ARENA_B // 2], BF16))
        PS = [st.enter_context(nc.psum_tensor("ps%d" % i, [128, 512], F32)) for i in range(8)]

        class Alloc:
            def __init__(self, lo, hi):
                self.lo, self.hi, self.p = lo, hi, lo

            def get(self, shape, dt):
                n = int(np.prod(shape)) * (4 if dt == F32 else 2)
                n = (n + 31) // 32 * 32
                off = self.p
                self.p += n
                assert self.p <= self.hi, ("SBUF arena overflow", self.p, self.hi)
                v = arena[:, off // 2:(off + n) // 2]
                if dt == F32:
                    v = v.bitcast(F32)
                ne = int(np.prod(shape))
                v = v[:, 0:ne]
                if len(shape) == 2:
                    v = v.rearrange("p (a b) -> p a b", a=shape[0])
                elif len(shape) == 3:
                    v = v.rearrange("p (a b c) -> p a b c", a=shape[0], b=shape[1])
                return v

        perm = Alloc(0, ARENA_B)
        identv = perm.get([128], BF16)
        onesv = perm.get([128], BF16)
        sst = perm.get([L * 2 * NSUB], F32)
        rstd1 = perm.get([NSUB], F32)
        rstd2 = perm.get([NSUB], F32)
        ssp = perm.get([NS * (D // 512)], F32)
        seqm = perm.get([NQB * NKT], F32)
        rowm = perm.get([NRM], F32)
        gainsv = perm.get([6, 128], F32)
        lamv = perm.get([4, 128], F32)
        sublnv = perm.get([2], F32)
        smallv = perm.get([64], F32)
        colmv = perm.get([64], F32)
        PERM_END = perm.p

        def dma(out, in_):
            return lambda e: e.dma_start(out=out, in_=in_)

        def mm(out, lhsT, rhs, a, b):
            return lambda e: e.matmul(out=out, lhsT=lhsT, rhs=rhs, start=a, stop=b)

        def tr(out, in_):
            return lambda e: e.transpose(out=out, in_=in_, identity=identv)

        def act(out, in_, func, bias=None, scale=None, accum=None):
            kw = {}
            if bias is not None:
                kw["bias"] = bias
            if scale is not None:
                kw["scale"] = scale
            if accum is not None:
                kw["accum_out"] = accum
            return lambda e: e.activation(out=out, in_=in_, func=func, **kw)

        def tt(out, a, b, op):
            return lambda e: e.tensor_tensor(out=out, in0=a, in1=b, op=op)

        def stt(out, a, sc, b, op0, op1):
            return lambda e: e.scalar_tensor_tensor(out=out, in0=a, scalar=sc, in1=b, op0=op0, op1=op1)

        def ts(out, a, s1, s2, op0, op1=None):
            if op1 is None:
                return lambda e: e.tensor_scalar(out=out, in0=a, scalar1=s1, scalar2=None, op0=op0)
            return lambda e: e.tensor_scalar(out=out, in0=a, scalar1=s1, scalar2=s2, op0=op0, op1=op1)

        def cp(eng, out, in_):
            if eng == "act":
                return lambda e: e.copy(out=out, in_=in_)
            return lambda e: e.tensor_copy(out=out, in_=in_)

        def tpview(i):
            return PS[i][:].bitcast(BF16)[:, 0:512].rearrange("p (a b) -> p a b", a=4)

        S.add("sync", dma(identv, ident_in), writes=["ident"], dma_key="c0")
        S.add("sync", dma(onesv, ones_in), writes=["ones"], dma_key="c1")
        S.add("sync", dma(seqm, seqmask_in), writes=["seqm"], dma_key="c2")
        S.add("sync", dma(rowm, rowmask_in), writes=["rowm"], dma_key="c3")
        S.add("sync", dma(colmv, colmask_in), writes=["colm"], dma_key="c4")
        S.add("dve", lambda e: e.memset(sst, 0.0), writes=["sst"])

        cvk = [0]

        def convert(dst, src, name, l):
            R = src.shape[0]
            for r0 in range(0, R, 128):
                S.add("pool", dma(dst[r0:r0 + 128, :], src[r0:r0 + 128, :]), writes=[("wb", name, l, r0 // 128)],
                      dma_key=("cv", cvk[0] % 8), nobar=True)
                cvk[0] += 1

        def convert_o(l):
            r = 0
            for nm, src in (("oa", w_oa[l]), ("ob", w_ob[l]), ("oc", w_oc[l])):
                R = src.shape[0]
                for r0 in range(0, R, 128):
                    S.add("pool", dma(wb_o[l][r + r0:r + r0 + 128, :], src[r0:r0 + 128, :]),
                          writes=[("wb", "o", l, (r + r0) // 128)], dma_key=("cv", cvk[0] % 8), nobar=True)
                    cvk[0] += 1
                r += R

        for l in range(L):
            convert(wb_in[l], w_in[l], "in", l)
            convert_o(l)
            convert(wb_out[l], w_out[l], "out", l)
            convert(wb_up[l], w_up[l], "up", l)
            convert(wb_down[l], w_down[l], "down", l)

        slab_i = [0]
        SLABS = None

        def slab_load(mat, name, l, krow0, col0):
            slot = slab_i[0] % 2
            slab_i[0] += 1
            S.add("sync", dma(SLABS[slot], mat[krow0:krow0 + D, col0:col0 + 512].rearrange("(c p) n -> p c n", p=128)),
                  reads=[("wb", name, l, krow0 // 128 + c) for c in range(KC)], writes=[("slab", slot)],
                  dma_key=("slab", slot))
            return slot

        def run_steps(steps):
            slots = [None] * len(steps)
            if steps:
                slots[0] = slab_load(*steps[0][0])
            for j, (spec, fn) in enumerate(steps):
                if j + 1 < len(steps):
                    slots[j + 1] = slab_load(*steps[j + 1][0])
                fn(slots[j])

        rot = {}

        def nxt(name, n):
            v = rot.get(name, 0)
            rot[name] = v + 1
            return v % n

        coltab = []
        qk_base = {"qa": 0, "ka": HA, "qb": 2 * HA, "kb": 2 * HA + 2 * HB, "qc": 2 * HA + 4 * HB, "kc": 2 * HA + 4 * HB + HQ}
        v_base = {"va": 0, "vb": HA * 128, "vc": HA * 128 + HB * 256}
        g_base = {"ga": 0, "gb": D, "gc": 2 * D}
        gain_idx = {"qa": 0, "ka": 1, "qb": 2, "kb": 3, "qc": 4, "kc": 5}
        for nm, w in cfg["segs"]:
            for i in range(w // 512):
                coltab.append((nm, i))
        assert len(coltab) * 512 == NIN

        for l in range(L):
            lam_init = 0.8 - 0.6 * math.exp(-0.3 * l)
            x_src = x_in if l == 0 else XB
            x_dst_final = y_out if l == L - 1 else XB

            S.barrier()
            A = Alloc(PERM_END, ARENA_B)
            SLABS = [A.get([KC, 512], BF16) for _ in range(2)]
            hT = A.get([KC, 512], BF16)
            gbc = A.get([D], F32)
            xt = [A.get([D], F32) for _ in range(2)]
            hb = [A.get([D], BF16) for _ in range(2)]
            ropev = A.get([NS, 4 * 128], F32)
            stg = [A.get([4, 512], BF16) for _ in range(3)]
            tmpf = [A.get([512], F32) for _ in range(7)]
            xbb = [A.get([512], BF16) for _ in range(2)]
            S.add("sync", dma(gbc, g1[l]), writes=["gbc"], dma_key="gbc")
            S.add("sync", dma(gainsv.rearrange("p a b -> p (a b)"), gains_in[l]), writes=["gains"], dma_key="c0")

            def tmp():
                i = nxt("tmp", 7)
                return tmpf[i], ("tmp", i)

            def a1(ttile, l=l):
                S.add("sync", dma(ropev.rearrange("p s k -> p s k"), rope_in[ttile * T:(ttile + 1) * T, :].rearrange("(s p) k -> p s k", p=128)),
                      writes=["rope"], dma_key="rope")
                for s in range(NS):
                    b = s % 2
                    tok0 = ttile * T + s * 128
                    sub = ttile * NS + s
                    col = l * 2 * NSUB + sub
                    S.add("sync", dma(xt[b], x_src[tok0:tok0 + 128, :]), reads=[("xb", sub, n) for n in range(D // 512)],
                          writes=[("xt", b)], dma_key=("xt", b))
                    S.add("act", act(hb[b], xt[b], AF.Square, accum=sst[:, col:col + 1]), reads=[("xt", b), "sst"],
                          writes=[("hb", b), ("sstc", col)])
                    S.add("act", act(smallv[:, 0:1], sst[:, col:col + 1], AF.Ln, bias=EPS, scale=1.0 / D),
                          reads=[("sstc", col)], writes=["small0"])
                    S.add("act", act(rstd1[:, sub:sub + 1], smallv[:, 0:1], AF.Exp, scale=-0.5), reads=["small0"],
                          writes=[("rstd1", sub)])
                    S.add("dve", stt(hb[b], xt[b], rstd1[:, sub:sub + 1], gbc, ALU.mult, ALU.mult),
                          reads=[("xt", b), ("rstd1", sub), "gbc"], writes=[("hb", b)])
                    for c4 in range(KC // 4):
                        tb = 4 + nxt("tp", 2)
                        tv = tpview(tb)
                        for j in range(4):
                            c = c4 * 4 + j
                            S.add("pe", tr(tv[:, j, :], hb[b][:, c * 128:(c + 1) * 128]), reads=[("hb", b), "ident"],
                                  writes=[("ps", tb)])
                        eng = "act" if c4 % 2 else "dve"
                        S.add(eng, cp(eng, hT[:, c4 * 4:(c4 + 1) * 4, s * 128:(s + 1) * 128], tv), reads=[("ps", tb)],
                              writes=["hT"])

            def a2_step(ttile, nb, slot, l=l):
                nm, bi = coltab[nb]
                slab = SLABS[slot]
                kind = nm[0]
                if kind == "g":
                    si = nxt("stg", 3)
                    sg = stg[si]
                    for f in range(4):
                        a = nxt("acc", 4)
                        for kc in range(KC):
                            S.add("pe", mm(PS[a][:], slab[:, kc, f * 128:(f + 1) * 128], hT[:, kc, :], kc == 0, kc == KC - 1),
                                  reads=["hT", ("slab", slot)], writes=[("ps", a)])
                        S.add("act", act(sg[:, f, :], PS[a][:], AF.Sigmoid), reads=[("ps", a)], writes=[("stg", si)])
                    r0 = g_base[nm] + bi * 512
                    S.add("act", dma(GS[r0:r0 + 512, ttile * T:(ttile + 1) * T].rearrange("(f p) t -> p f t", p=128), sg),
                          reads=[("stg", si)], dma_key=("stg", si))
                    return
                si = nxt("stg", 3)
                sg = stg[si]
                for s in range(NS):
                    a = nxt("acc", 4)
                    for kc in range(KC):
                        S.add("pe", mm(PS[a][:], hT[:, kc, s * 128:(s + 1) * 128], slab[:, kc, :], kc == 0, kc == KC - 1),
                              reads=["hT", ("slab", slot)], writes=[("ps", a)])
                    if kind == "v":
                        S.add("act", cp("act", sg[:, s, :], PS[a][:]), reads=[("ps", a)], writes=[("stg", si)])
                        continue
                    xq, rq = tmp()
                    S.add("act", cp("act", xq, PS[a][:]), reads=[("ps", a)], writes=[rq])
                    sq, rsq = tmp()
                    S.add("dve", tt(sq, xq, xq, ALU.mult), reads=[rq], writes=[rsq])
                    S.add("dve", lambda e, sq=sq: e.tensor_reduce(out=smallv[:, 4:8], in_=sq.rearrange("p (h d) -> p h d", h=4),
                                                                 axis=AX.X, op=ALU.add), reads=[rsq], writes=["small1"])
                    S.add("act", act(smallv[:, 8:12], smallv[:, 4:8], AF.Ln, bias=EPS, scale=1.0 / 128), reads=["small1"],
                          writes=["small2"])
                    S.add("act", act(smallv[:, 12:16], smallv[:, 8:12], AF.Exp, scale=-0.5), reads=["small2"], writes=["small3"])
                    xq3 = xq.rearrange("p (h d) -> p h d", h=4)
                    S.add("dve", tt(xq3, xq3, smallv[:, 12:16].unsqueeze(2).to_broadcast([128, 4, 128]), ALU.mult),
                          reads=[rq, "small3"], writes=[rq])
                    gi = gain_idx[nm]
                    gb_ = gainsv[:, gi, :].unsqueeze(1).to_broadcast([128, 4, 128])
                    bi_ = nxt("xbb", 2)
                    xb_ = xbb[bi_]
                    xb3 = xb_.rearrange("p (h d) -> p h d", h=4)
                    if nm[1] == "a":
                        S.add("dve", tt(xb3, xq3, gb_, ALU.mult), reads=[rq, "gains"], writes=[("xbb", bi_)])
                    else:
                        k0 = 0 if nm[1] == "b" else 2
                        GG, dd = (1, 64) if nm[1] == "b" else (2, 32)
                        S.add("dve", tt(xq3, xq3, gb_, ALU.mult), reads=[rq, "gains"], writes=[rq])
                        xr, rxr = tmp()
                        xr3 = xr.rearrange("p (h d) -> p h d", h=4)
                        cosb = ropev[:, s, k0 * 128:(k0 + 1) * 128].unsqueeze(1).to_broadcast([128, 4, 128])
                        S.add("dve", tt(xr3, xq3, cosb, ALU.mult), reads=[rq, "rope"], writes=[rxr])
                        t2, rt2 = tmp()
                        sinv = ropev[:, s, (k0 + 1) * 128:(k0 + 2) * 128].rearrange("p (g two d) -> p g two d", g=GG, two=2)
                        x5 = xq.rearrange("p (h g two d) -> p h g two d", h=4, g=GG, two=2)
                        t5 = t2.rearrange("p (h g two d) -> p h g two d", h=4, g=GG, two=2)
                        for o_, i_ in ((0, 1), (1, 0)):
                            sb_ = sinv[:, :, o_, :].unsqueeze(1).to_broadcast([128, 4, GG, dd])
                            S.add("dve", tt(t5[:, :, :, o_, :], x5[:, :, :, i_, :], sb_, ALU.mult), reads=[rq, "rope"],
                                  writes=[rt2])
                        S.add("dve", tt(xb_, xr, t2, ALU.add), reads=[rxr, rt2], writes=[("xbb", bi_)])
                    tb = 4 + nxt("tp", 2)
                    tv = tpview(tb)
                    for h in range(4):
                        S.add("pe", tr(tv[:, h, :], xb_[:, h * 128:(h + 1) * 128]), reads=[("xbb", bi_), "ident"],
                              writes=[("ps", tb)])
                    eng = "act" if s % 2 else "dve"
                    S.add(eng, cp(eng, sg[:, :, s * 128:(s + 1) * 128], tv), reads=[("ps", tb)], writes=[("stg", si)])
                if kind == "v":
                    c0 = v_base[nm] + bi * 512
                    S.add("act", dma(VS[ttile * T:(ttile + 1) * T, c0:c0 + 512].rearrange("(s p) c -> p s c", p=128), sg),
                          reads=[("stg", si)], dma_key=("stg", si))
                else:
                    h0 = qk_base[nm] + bi * 4
                    S.add("act", dma(QT[h0:h0 + 4, :, ttile * T:(ttile + 1) * T].rearrange("h d t -> d h t"), sg),
                          reads=[("stg", si)], dma_key=("stg", si))

            steps = []
            for ttile in range(NTT):
                for nb in range(NIN // 512):
                    def fn(slot, ttile=ttile, nb=nb):
                        if nb == 0:
                            a1(ttile)
                        a2_step(ttile, nb, slot)
                    steps.append(((wb_in[l], "in", l, 0, nb * 512), fn))
            run_steps(steps)

            S.barrier()
            Bm = Alloc(PERM_END, ARENA_B)
            tabv = Bm.get([HA * 14, 64], F32)
            kTb = [Bm.get([NT], BF16) for _ in range(3)]
            qTb = [Bm.get([NT], BF16) for _ in range(3)]
            vEb = [Bm.get([NKT, 256], BF16) for _ in range(2)]
            vOb = [Bm.get([NKT, 128], BF16) for _ in range(2)]
            oab = [Bm.get([NT], BF16) for _ in range(2)]
            ptl = [Bm.get([512], BF16) for _ in range(6)]
            tmpb = [Bm.get([512], F32) for _ in range(12)]
            ostg = [Bm.get([2, 512], BF16) for _ in range(3)]
            scale = 128 ** -0.5

            def tmpB():
                i = nxt("tmpb", 12)
                return tmpb[i], ("tmpb", i)

            def ptile():
                i = nxt("pt", 6)
                return ptl[i], ("pt", i)

            S.add("sync", dma(tabv.rearrange("p a b -> p (a b)"), tab_in[l]), writes=["tab"], dma_key="tab")
            S.add("dve", tt(tabv, tabv, colmv.unsqueeze(1).to_broadcast([128, HA * 14, 64]), ALU.add), reads=["tab", "colm"],
                  writes=["tab"])
            S.add("sync", dma(lamv.rearrange("p a b -> p (a b)"), lamv_in[l]), writes=["lamv"], dma_key="c1")
            S.add("sync", dma(sublnv, subln_in[l]), writes=["subln"], dma_key="c2")
            t0_, r0_ = tmpB()
            S.add("dve", tt(t0_[:, 0:128], lamv[:, 0, :], lamv[:, 1, :], ALU.mult), reads=["lamv"], writes=[r0_])
            S.add("dve", tt(t0_[:, 128:256], lamv[:, 2, :], lamv[:, 3, :], ALU.mult), reads=["lamv"], writes=[r0_])
            S.add("dve", lambda e, t0_=t0_: e.tensor_reduce(out=smallv[:, 20:22], in_=t0_[:, 0:256].rearrange("p (a d) -> p a d", a=2),
                                                           axis=AX.X, op=ALU.add), reads=[r0_], writes=["small5"])
            S.add("act", act(smallv[:, 22:24], smallv[:, 20:22], AF.Exp), reads=["small5"], writes=["small6"])
            S.add("dve", tt(smallv[:, 16:17], smallv[:, 23:24], smallv[:, 22:23], ALU.subtract), reads=["small6"], writes=["neglam"])
            S.add("dve", ts(smallv[:, 16:17], smallv[:, 16:17], -lam_init, None, ALU.add), reads=["neglam"], writes=["neglam"])

            ldk = [0]

            def load_head(buf_list, idx, src_ap, nm):
                i = nxt(nm, len(buf_list))
                S.add("sync", dma(buf_list[i] if src_ap is None else buf_list[i], src_ap), writes=[(nm, i)], dma_key=(nm, i))
                return buf_list[i], (nm, i)

            for h in range(HA):
                kT, rk = load_head(kTb, 0, QT[qk_base["ka"] + h], "kT")
                qT, rqh = load_head(qTb, 0, QT[qk_base["qa"] + h], "qT")
                iv = nxt("vE", 2)
                vE = vEb[iv][:, :, 0:128]
                S.add("sync", dma(vE, VS[:, h * 128:(h + 1) * 128].rearrange("(k p) d -> p k d", p=128)), writes=[("vE", iv)],
                      dma_key=("vE", iv))
                io = nxt("vO", 2)
                vO = vOb[io]
                S.add("sync", dma(vO[:, 0:NKT - 1, :], VS[64:64 + (NKT - 1) * 128, h * 128:(h + 1) * 128].rearrange("(k p) d -> p k d", p=128)),
                      writes=[("vO", io)], dma_key=("vO", io))
                ioa = nxt("oa", 2)
                oa = oab[ioa]

                def stage1(w, kT=kT, qT=qT, rk=rk, rqh=rqh, h=h):
                    r, lo, nt = w["r"], w["lo"], w["nt"]
                    a = nxt("sps", 3)
                    for t in range(nt):
                        S.add("pe", mm(PS[a][:, t * 64:(t + 1) * 64], kT[:, (lo + 2 * t) * 64:(lo + 2 * t) * 64 + 128],
                                       qT[:, r * 64:(r + 1) * 64], True, True), reads=[rk, rqh], writes=[("ps", a)])
                    sb_, rsb = tmpB()
                    d0 = w["dr0"]
                    tabs = tabv.rearrange("p (h r) c -> p h r c", h=HA)[:, h, d0:d0 + 2 * (nt - 1) + 1:2, :]
                    S.add("dve", stt(sb_[:, 0:nt * 64].rearrange("p (t c) -> p t c", t=nt),
                                     PS[a][:, 0:nt * 64].rearrange("p (t c) -> p t c", t=nt), scale, tabs, ALU.mult, ALU.add),
                          reads=[("ps", a), "tab"], writes=[rsb])
                    p_, rp = ptile()
                    if not w["special"]:
                        S.add("act", act(p_[:, 0:nt * 64], sb_[:, 0:nt * 64], AF.Exp), reads=[rsb], writes=[rp])
                    else:
                        for t in range(nt):
                            c = w["col"] + t
                            S.add("act", act(p_[:, t * 64:(t + 1) * 64], sb_[:, t * 64:(t + 1) * 64], AF.Exp, bias=rowm[:, c:c + 1]),
                                  reads=[rsb, "rowm"], writes=[rp])
                    return p_, rp

                def stage2(w, p_, rp, vE=vE, vO=vO, iv=iv, io=io, oa=oa, ioa=ioa):
                    r, lo, nt = w["r"], w["lo"], w["nt"]
                    if lo % 2 == 0:
                        vt, rv, base = vE, ("vE", iv), lo // 2
                    else:
                        vt, rv, base = vO, ("vO", io), (lo - 1) // 2
                    po = 3 + nxt("aps", 2)
                    pd = 5 + nxt("apd", 2)
                    for t in range(nt):
                        S.add("pe", mm(PS[po][:, 0:64], vt[:, base + t, :], p_[:, t * 64:(t + 1) * 64], t == 0, t == nt - 1),
                              reads=[rv, rp], writes=[("ps", po)])
                    for t in range(nt):
                        S.add("pe", mm(PS[pd][:, 0:64], onesv, p_[:, t * 64:(t + 1) * 64], t == 0, t == nt - 1),
                              reads=["ones", rp], writes=[("ps", pd)])
                    rd, rrd = tmpB()
                    S.add("dve", lambda e, rd=rd, pd=pd: e.reciprocal(out=rd[:, 0:64], in_=PS[pd][:, 0:64]), reads=[("ps", pd)],
                          writes=[rrd])
                    S.add("dve", tt(oa[:, r * 64:(r + 1) * 64], PS[po][:, 0:64], rd[:, 0:64], ALU.mult), reads=[("ps", po), rrd],
                          writes=[("oa", ioa)])

                pend = stage1(wins[0])
                for ri in range(rows):
                    nx = stage1(wins[ri + 1]) if ri + 1 < rows else None
                    stage2(wins[ri], *pend)
                    pend = nx
                S.add("act", dma(OT[h * 128:(h + 1) * 128, :], oa), reads=[("oa", ioa)], dma_key=("oa", ioa))

            def attn_unit(kT, rk, vt, rv, nh, qT, rqh, qb, evac):
                def pv(kt, p_, rp):
                    for e_ in range(nh):
                        S.add("pe", mm(PS[3 + e_][:], vt[:, kt, e_ * 128:(e_ + 1) * 128], p_, kt == 0, kt == NKT - 1),
                              reads=[rv, rp], writes=[("ps", 3 + e_)])
                    S.add("pe", mm(PS[5][:], onesv, p_, kt == 0, kt == NKT - 1), reads=["ones", rp], writes=[("ps", 5)])

                pend = None
                for kt in range(NKT):
                    a = nxt("sps", 3)
                    S.add("pe", mm(PS[a][:], kT[:, kt * 128:(kt + 1) * 128], qT[:, qb * 512:(qb + 1) * 512], True, True),
                          reads=[rk, rqh], writes=[("ps", a)])
                    p_, rp = ptile()
                    c = qb * NKT + kt
                    S.add("act", act(p_, PS[a][:], AF.Exp, bias=seqm[:, c:c + 1], scale=scale), reads=[("ps", a), "seqm"],
                          writes=[rp])
                    if pend is not None:
                        pv(*pend)
                    pend = (kt, p_, rp)
                pv(*pend)
                rd, rrd = tmpB()
                S.add("dve", lambda e, rd=rd: e.reciprocal(out=rd, in_=PS[5][:]), reads=[("ps", 5)], writes=[rrd])
                evac(rd, rrd)

            ob_row0 = HA * 128
            for hb_ in range(HB):
                iv = nxt("vE", 2)
                c0 = v_base["vb"] + hb_ * 256
                S.add("sync", dma(vEb[iv], VS[:, c0:c0 + 256].rearrange("(k p) d -> p k d", p=128)), writes=[("vE", iv)],
                      dma_key=("vE", iv))
                ks, qs = [], []
                for pr in range(2):
                    ks.append(load_head(kTb, 0, QT[qk_base["kb"] + hb_ * 2 + pr], "kT"))
                    qs.append(load_head(qTb, 0, QT[qk_base["qb"] + hb_ * 2 + pr], "qT"))
                for qb in range(NQB):
                    on = []
                    for pr in range(2):
                        dst = [tmpB(), tmpB()]

                        def evac(rd, rrd, dst=dst):
                            for e_ in range(2):
                                S.add("dve", tt(dst[e_][0], PS[3 + e_][:], rd, ALU.mult), reads=[("ps", 3 + e_), rrd],
                                      writes=[dst[e_][1]])
                        attn_unit(ks[pr][0], ks[pr][1], vEb[iv], ("vE", iv), 2, qs[pr][0], qs[pr][1], qb, evac)
                        on.append(dst)
                    for e_ in range(2):
                        S.add("dve", stt(on[0][e_][0], on[1][e_][0], smallv[:, 16:17], on[0][e_][0], ALU.mult, ALU.add),
                              reads=[on[1][e_][1], on[0][e_][1], "neglam"], writes=[on[0][e_][1]])
                        sq, rsq = ptile()
                        S.add("dve", tt(sq, on[0][e_][0], on[0][e_][0], ALU.mult), reads=[on[0][e_][1]], writes=[rsq])
                        S.add("pe", mm(PS[6][:], onesv, sq, e_ == 0, e_ == 1), reads=["ones", rsq], writes=[("ps", 6)])
                    tl, rtl = tmpB()
                    S.add("act", act(tl, PS[6][:], AF.Ln, bias=EPS, scale=1.0 / 256), reads=[("ps", 6)], writes=[rtl])
                    S.add("act", act(tl, tl, AF.Exp, bias=math.log(1.0 - lam_init), scale=-0.5), reads=[rtl], writes=[rtl])
                    io_ = nxt("ostg", 3)
                    for e_ in range(2):
                        S.add("dve", stt(ostg[io_][:, e_, :], on[0][e_][0], sublnv[:, e_:e_ + 1], tl, ALU.mult, ALU.mult),
                              reads=[on[0][e_][1], rtl, "subln"], writes=[("ostg", io_)])
                    r0 = ob_row0 + hb_ * 256
                    S.add("act", dma(OT[r0:r0 + 256, qb * 512:(qb + 1) * 512].rearrange("(e p) t -> p e t", p=128), ostg[io_]),
                          reads=[("ostg", io_)], dma_key=("ostg", io_))

            oc_row0 = HA * 128 + HB * 256
            for n in range(HKV):
                iv = nxt("vE", 2)
                c0 = v_base["vc"] + n * 128
                vC = vEb[iv][:, :, 0:128]
                S.add("sync", dma(vC, VS[:, c0:c0 + 128].rearrange("(k p) d -> p k d", p=128)), writes=[("vE", iv)],
                      dma_key=("vE", iv))
                kT, rk = load_head(kTb, 0, QT[qk_base["kc"] + n], "kT")
                for g_ in range(G):
                    hq = n * G + g_
                    qT, rqh = load_head(qTb, 0, QT[qk_base["qc"] + hq], "qT")
                    for qb in range(NQB):
                        io_ = nxt("ostg", 3)

                        def evac(rd, rrd, io_=io_):
                            S.add("dve", tt(ostg[io_][:, 0, :], PS[3][:], rd, ALU.mult), reads=[("ps", 3), rrd],
                                  writes=[("ostg", io_)])
                        attn_unit(kT, rk, vC, ("vE", iv), 1, qT, rqh, qb, evac)
                        r0 = oc_row0 + hq * 128
                        S.add("act", dma(OT[r0:r0 + 128, qb * 512:(qb + 1) * 512], ostg[io_][:, 0, :]), reads=[("ostg", io_)],
                              dma_key=("ostg", io_))

            S.barrier()
            C = Alloc(PERM_END, ARENA_B)
            SLABS = [C.get([KC, 512], BF16) for _ in range(2)]
            bufA = C.get([KC, 512], BF16)
            gbc = C.get([D], F32)
            bufB = C.get([KSH * KC, 512], BF16)
            gtl = [C.get([3, 512], BF16) for _ in range(2)]
            tmpc = [C.get([512], F32) for _ in range(6)]
            hb2 = [C.get([512], BF16) for _ in range(2)]
            mT = bufB[:, 0:KC, :]
            uT = bufB
            S.add("sync", dma(gbc, g2[l]), writes=["gbc"], dma_key="gbc")

            def tmpC():
                i = nxt("tmpc", 6)
                return tmpc[i], ("tmpc", i)

            na, nb_, ncc = HA, 2 * HB, HQ
            assert na + nb_ + ncc == KC

            steps = []
            for ttile in range(NTT):
                tsl = slice(ttile * T, (ttile + 1) * T)

                for nb in range(D // 512):
                    def fn(slot, nb=nb, ttile=ttile, tsl=tsl):
                        slab = SLABS[slot]
                        if nb == 0:
                            S.add("sync", dma(bufA, OT[:, tsl].rearrange("(c p) t -> p c t", p=128)), writes=["bufA"], dma_key="bufA")
                        for f in range(4):
                            ft = nb * 4 + f
                            ig = nxt("gt", 2)
                            S.add("sync", dma(gtl[ig], GS[:, tsl].rearrange("(g q) t -> q g t", g=3)[ft * 128:(ft + 1) * 128]),
                                  writes=[("gt", ig)], dma_key=("gt", ig))
                            base = 3 * nxt("oset", 2)
                            for bi, (k0, k1) in enumerate(((0, na), (na, na + nb_), (na + nb_, KC))):
                                for kc in range(k0, k1):
                                    S.add("pe", mm(PS[base + bi][:], slab[:, kc, f * 128:(f + 1) * 128], bufA[:, kc, :], kc == k0,
                                                   kc == k1 - 1), reads=["bufA", ("slab", slot)], writes=[("ps", base + bi)])
                            ms = [tmpC() for _ in range(3)]
                            for bi in range(3):
                                S.add("dve", tt(ms[bi][0], PS[base + bi][:], gtl[ig][:, bi, :], ALU.mult),
                                      reads=[("ps", base + bi), ("gt", ig)], writes=[ms[bi][1]])
                            S.add("dve", tt(ms[0][0], ms[0][0], ms[1][0], ALU.add), reads=[ms[0][1], ms[1][1]], writes=[ms[0][1]])
                            S.add("dve", tt(mT[:, ft, :], ms[0][0], ms[2][0], ALU.add), reads=[ms[0][1], ms[2][1]], writes=["bufB"])
                    steps.append(((wb_o[l], "o", l, 0, nb * 512), fn))

                for nb in range(D // 512):
                    def fn(slot, nb=nb, ttile=ttile, l=l):
                        slab = SLABS[slot]
                        if nb == 0:
                            S.add("dve", lambda e: e.memset(ssp, 0.0),
                                  writes=[("ssp", s_, n_) for s_ in range(NS) for n_ in range(D // 512)])
                        for s in range(NS):
                            sub = ttile * NS + s
                            tok0 = sub * 128
                            a = nxt("acc", 4)
                            for kc in range(KC):
                                S.add("pe", mm(PS[a][:], mT[:, kc, s * 128:(s + 1) * 128], slab[:, kc, :], kc == 0, kc == KC - 1),
                                      reads=["bufB", ("slab", slot)], writes=[("ps", a)])
                            xin, rxi = tmpC()
                            S.add("sync", dma(xin, x_src[tok0:tok0 + 128, nb * 512:(nb + 1) * 512]), reads=[("xb", sub, nb)],
                                  writes=[rxi], dma_key=rxi)
                            S.add("dve", tt(xin, PS[a][:], xin, ALU.add), reads=[("ps", a), rxi], writes=[rxi])
                            S.add("act", dma(XB[tok0:tok0 + 128, nb * 512:(nb + 1) * 512], xin), reads=[rxi], writes=[("xb", sub, nb)],
                                  dma_key=rxi)
                            col = l * 2 * NSUB + NSUB + sub
                            ih = nxt("hb2", 2)
                            S.add("act", act(hb2[ih], xin, AF.Square, accum=ssp[:, s * (D // 512) + nb:s * (D // 512) + nb + 1]),
                                  reads=[rxi], writes=[("hb2", ih), ("ssp", s, nb)])
                            S.add("dve", tt(hb2[ih], xin, gbc[:, nb * 512:(nb + 1) * 512], ALU.mult), reads=[rxi, "gbc"],
                                  writes=[("hb2", ih)])
                            tb = 4 + nxt("tp", 2)
                            tv = tpview(tb)
                            for j in range(4):
                                S.add("pe", tr(tv[:, j, :], hb2[ih][:, j * 128:(j + 1) * 128]), reads=[("hb2", ih), "ident"],
                                      writes=[("ps", tb)])
                            eng = "act" if s % 2 else "dve"
                            S.add(eng, cp(eng, bufA[:, nb * 4:(nb + 1) * 4, s * 128:(s + 1) * 128], tv), reads=[("ps", tb)],
                                  writes=["bufA"])
                        if nb == D // 512 - 1:
                            S.add("dve", lambda e: e.tensor_reduce(out=smallv[:, 32:32 + NS], in_=ssp.rearrange("p (s n) -> p s n", s=NS),
                                                                   axis=AX.X, op=ALU.add),
                                  reads=[("ssp", s_, n_) for s_ in range(NS) for n_ in range(D // 512)], writes=["small8"])
                            S.add("dve", ts(smallv[:, 32:32 + NS], smallv[:, 32:32 + NS], 1.0 / D, EPS, ALU.mult, ALU.add),
                                  reads=["small8"], writes=["small8"])
                            S.add("dve", lambda e, ttile=ttile: e.reciprocal(out=rstd2[:, ttile * NS:(ttile + 1) * NS],
                                                                            in_=smallv[:, 32:32 + NS]), reads=["small8"],
                                  writes=[("rstd2", ttile)])
                    steps.append(((wb_out[l], "out", l, 0, nb * 512), fn))

                for half in range(2):
                    for nb in range(NUPB):
                        def fn(slot, nb=nb):
                            slab = SLABS[slot]
                            for f in range(4):
                                a = nxt("acc", 4)
                                for kc in range(KC):
                                    S.add("pe", mm(PS[a][:], slab[:, kc, f * 128:(f + 1) * 128], bufA[:, kc, :], kc == 0, kc == KC - 1),
                                          reads=["bufA", ("slab", slot)], writes=[("ps", a)])
                                r_, rr = tmpC()
                                S.add("act", act(r_, PS[a][:], AF.Relu), reads=[("ps", a)], writes=[rr])
                                S.add("dve", tt(uT[:, nb * 4 + f, :], r_, r_, ALU.mult), reads=[rr], writes=["bufB"])
                        steps.append(((wb_up[l], "up", l, 0, half * (FF // 2) + nb * 512), fn))
                    for nb in range(D // 512):
                        for ks in range(KSH):
                            def fn(slot, nb=nb, ks=ks, half=half, ttile=ttile):
                                slab = SLABS[slot]
                                if ks == 0:
                                    rot["dset_cur"] = 4 * nxt("dset", 2)
                                base = rot["dset_cur"]
                                for s in range(NS):
                                    for kc in range(KC):
                                        S.add("pe", mm(PS[base + s][:], uT[:, ks * KC + kc, s * 128:(s + 1) * 128], slab[:, kc, :],
                                                       ks == 0 and kc == 0, ks == KSH - 1 and kc == KC - 1),
                                              reads=["bufB", ("slab", slot)], writes=[("ps", base + s)])
                                if ks == KSH - 1:
                                    for s in range(NS):
                                        sub = ttile * NS + s
                                        tok0 = sub * 128
                                        yin, ryi = tmpC()
                                        S.add("sync", dma(yin, XB[tok0:tok0 + 128, nb * 512:(nb + 1) * 512]), reads=[("xb", sub, nb)],
                                              writes=[ryi], dma_key=ryi)
                                        S.add("dve", stt(yin, PS[base + s][:], rstd2[:, sub:sub + 1], yin, ALU.mult, ALU.add),
                                              reads=[("ps", base + s), ryi, ("rstd2", ttile)], writes=[ryi])
                                        dst = XB if half == 0 else x_dst_final
                                        S.add("act", dma(dst[tok0:tok0 + 128, nb * 512:(nb + 1) * 512], yin), reads=[ryi],
                                              writes=[("xb", sub, nb)], dma_key=ryi)
                            steps.append(((wb_down[l], "down", l, half * (FF // 2) + ks * D, nb * 512), fn))
            run_steps(steps)

        S.barrier()
        S.finalize(nc, st)
        S.emit_all(nc)
    return nc


def host_tables(cfg, is_prompt):
    NT = cfg["NT"]
    NKT, NQB = NT // 128, NT // 512
    wins, NRM = windows(cfg)
    seqlen = NT if is_prompt else NT // 2
    pos = (np.arange(NT) % seqlen).astype(np.float32)
    invB = (10000.0 ** (-np.arange(0, 128, 2, dtype=np.float32) / 128)).astype(np.float32)
    angB = pos[:, None] * invB[None, :]
    cB, sB = np.cos(angB), np.sin(angB)
    inv64 = (10000.0 ** (-np.arange(0, 64, 2, dtype=np.float32) / 64)).astype(np.float32)
    pr = np.floor(pos / 64.0).astype(np.float32)
    pc = (pos - pr * 64.0).astype(np.float32)
    a1, a2 = pr[:, None] * inv64[None, :], pc[:, None] * inv64[None, :]
    rope = np.zeros((NT, 4, 128), np.float32)
    rope[:, 0] = np.concatenate([cB, cB], 1)
    rope[:, 1] = np.concatenate([-sB, sB], 1)
    rope[:, 2] = np.concatenate([np.cos(a1), np.cos(a1), np.cos(a2), np.cos(a2)], 1)
    rope[:, 3] = np.concatenate([-np.sin(a1), np.sin(a1), -np.sin(a2), np.sin(a2)], 1)
    seqmask = np.zeros((128, NQB * NKT), np.float32)
    for qb in range(NQB):
        for kt in range(NKT):
            if (qb * 512) // seqlen != (kt * 128) // seqlen:
                seqmask[:, qb * NKT + kt] = NEG
    rowmask = np.zeros((128, NRM), np.float32)
    for w in wins:
        if not w["special"]:
            continue
        lo_, hi_ = w["wp"] if is_prompt else w["ws"]
        for t in range(w["nt"]):
            for half in range(2):
                kr = w["lo"] + 2 * t + half
                if not (lo_ <= kr < hi_):
                    rowmask[half * 64:(half + 1) * 64, w["col"] + t] = NEG
    return rope.reshape(NT, 512), seqmask, rowmask


def host_common(cfg, inputs):
    L, D, HA = cfg["L"], cfg["D"], cfg["HA"]
    f = lambda a: np.ascontiguousarray(np.asarray(a, dtype=np.float32))
    rep = lambda a: np.ascontiguousarray(np.broadcast_to(f(a)[:, None, :], (L, 128, f(a).shape[-1])))
    m = {}
    for k in ("w_in", "w_oa", "w_ob", "w_oc", "w_out", "w_up", "w_down"):
        m[k] = f(inputs[k])
    m["g1"] = rep(inputs["norm_mix"])
    m["g2"] = rep(inputs["norm_mlp"])
    m["gains"] = np.ascontiguousarray(np.concatenate([rep(inputs[k]) for k in ("qn_a", "kn_a", "qn_b", "kn_b", "qn_c", "kn_c")], 2))
    m["lamv"] = np.ascontiguousarray(np.concatenate([rep(inputs[k]) for k in ("lam_q1", "lam_k1", "lam_q2", "lam_k2")], 2))
    m["subln"] = np.ascontiguousarray(f(inputs["subln_b"]).reshape(L, 2, 128).transpose(0, 2, 1))
    rpb = f(inputs["rpb"])
    p = np.arange(128)
    kc = p % 64
    up = (p >= 64).astype(np.int64)
    qc = np.arange(64)
    dc = np.clip(kc[:, None] - qc[None, :], -15, 15) + 15
    dr = np.arange(14)[None, :] + up[:, None]
    tab = rpb[:, :, dr[:, :, None], dc[:, None, :]]
    m["tab"] = np.ascontiguousarray(tab.transpose(0, 2, 1, 3, 4)).reshape(L, 128, HA * 14 * 64)
    cs = np.clip(qc - 8, 0, 48)
    ok = (kc[:, None] >= cs[None, :]) & (kc[:, None] < cs[None, :] + 16)
    m["colmask"] = np.where(ok, 0.0, NEG).astype(np.float32)
    m["ident"] = np.eye(128, dtype=np.float32).astype(ml_dtypes.bfloat16)
    m["ones"] = np.ones((128, 128), np.float32).astype(ml_dtypes.bfloat16)
    return m


_NC_CACHE = {}


def run(cfg, inputs, n_prompt):
    key = tuple(sorted((k, v) for k, v in cfg.items() if not isinstance(v, list)))
    if key not in _NC_CACHE:
        _NC_CACHE[key] = build(cfg)
    nc = _NC_CACHE[key]
    NT, D = cfg["NT"], cfg["D"]
    common = host_common(cfg, inputs)
    xp = np.asarray(inputs["x_prompt"], np.float32)
    xs = np.asarray(inputs["x_sample"], np.float32)
    tabs = {True: host_tables(cfg, True), False: host_tables(cfg, False)}
    in_maps = []
    ncores = 8
    for c in range(ncores):
        isp = c < n_prompt
        if isp:
            xc = xp[c].reshape(NT, D)
        else:
            j = c - n_prompt
            xc = xs[2 * j:2 * j + 2].reshape(NT, D)
        rope, seqmask, rowmask = tabs[isp]
        mm_ = dict(common)
        mm_.update(x=np.ascontiguousarray(xc), rope=rope, seqmask=seqmask, rowmask=rowmask)
        in_maps.append(mm_)
    res = run_bass_kernel_spmd(nc, in_maps, core_ids=list(range(ncores)))
    ys = [np.asarray(r["y"], np.float32) for r in res.results]
    yp = np.stack([ys[c] for c in range(n_prompt)], 0).reshape(xp.shape)
    ysamp = np.concatenate([ys[c].reshape(2, NT // 2, D) for c in range(n_prompt, ncores)], 0).reshape(xs.shape)
    return yp, ysamp


def kernel(**inputs):
    cfg = mkcfg()
    return run(cfg, inputs, 4)
```

```python
import contextlib
import math
import numpy as np
import ml_dtypes
import concourse.bass as bass
import concourse.mybir as mybir
from concourse.bass_utils import run_bass_kernel_spmd

F32 = mybir.dt.float32
BF16 = mybir.dt.bfloat16
AF = mybir.ActivationFunctionType
ALU = mybir.AluOpType
AX = mybir.AxisListType
ENGS = ("sync", "act", "dve", "pool", "pe")
NEG = -30000.0
EPS = 1e-6


class _Op:
    __slots__ = ("eng", "emit", "deps", "dma_key", "needed", "val", "semk", "i", "nobar")


class Sched:
    def __init__(self):
        self.ops = []
        self.last_writer = {}
        self.readers = {}
        self.last_dma = {}
        self.last_eng = {}

    def add(self, eng, emit, reads=(), writes=(), dma_key=None, nobar=False, extra=()):
        op = _Op()
        op.eng = eng
        op.emit = emit
        op.dma_key = dma_key
        op.needed = False
        op.nobar = nobar
        op.i = len(self.ops)
        deps = list(extra)
        for r in reads:
            w = self.last_writer.get(r)
            if w is not None:
                deps.append(w)
        for w_ in writes:
            w = self.last_writer.get(w_)
            if w is not None:
                deps.append(w)
            rd = self.readers.get(w_)
            if rd:
                deps.extend(rd.values())
        if dma_key is not None:
            p = self.last_dma.get(dma_key)
            if p is not None:
                deps.append(p)
            self.last_dma[dma_key] = op
        op.deps = deps
        for w_ in writes:
            self.last_writer[w_] = op
            self.readers[w_] = {}
        for r in reads:
            d = self.readers.setdefault(r, {})
            if dma_key is not None:
                d[("dma", op.i)] = op
            else:
                d[eng] = op
        if emit is not None and dma_key is None:
            self.last_eng[eng] = op
        self.ops.append(op)
        return op

    def barrier(self):
        deps = [o for o in self.last_eng.values()]
        deps += [o for o in self.last_dma.values() if not o.nobar]
        for e in ENGS:
            self.add(e, None, extra=deps)

    def finalize(self, nc, stack, epoch=30000):
        for op in self.ops:
            keep = []
            seen = set()
            for d in op.deps:
                if d.i in seen:
                    continue
                seen.add(d.i)
                if d.eng == "pe" and op.eng == "pe" and d.dma_key is None and op.dma_key is None and op.emit is not None:
                    continue
                keep.append(d)
                d.needed = True
            op.deps = keep
        sems = {}

        def getsem(k):
            if k not in sems:
                sems[k] = stack.enter_context(nc.semaphore("s%d" % len(sems)))
            return sems[k]

        cnt = {e: 0 for e in ENGS}
        ep = {e: 0 for e in ENGS}
        dcnt = {}
        for op in self.ops:
            if op.dma_key is not None:
                k = ("dma", op.dma_key)
                dcnt[k] = dcnt.get(k, 0) + 16
                op.semk = k
                op.val = dcnt[k]
            elif op.needed:
                if cnt[op.eng] >= epoch:
                    ep[op.eng] += 1
                    cnt[op.eng] = 0
                cnt[op.eng] += 1
                op.semk = ("eng", op.eng, ep[op.eng])
                op.val = cnt[op.eng]
            else:
                op.semk = None
                op.val = 0
        waited = {e: {} for e in ENGS}
        streams = {e: [] for e in ENGS}
        for op in self.ops:
            ws = {}
            for d in op.deps:
                if waited[op.eng].get(d.semk, 0) >= d.val:
                    continue
                if ws.get(d.semk, 0) < d.val:
                    ws[d.semk] = d.val
            for k, v in ws.items():
                waited[op.eng][k] = v
            inc = None
            if op.semk is not None:
                inc = (getsem(op.semk), 16 if op.dma_key is not None else 1)
            streams[op.eng].append(([(getsem(k), v) for k, v in ws.items()], op.emit, inc))
        self.streams = streams
        fin = []
        for k, v in list(dcnt.items()):
            fin.append((getsem(k), v))
        for e in ENGS:
            for epi in range(ep[e] + 1):
                k = ("eng", e, epi)
                if k in sems:
                    fin.append((sems[k], epoch if epi < ep[e] else cnt[e]))
        self.fin = fin
        self.nsems = len(sems)

    def emit_all(self, nc):
        with nc.Block() as block:
            def run(e, name):
                for waits, emit, inc in self.streams[name]:
                    for s, v in waits:
                        e.wait_ge(s, v)
                    if emit is None:
                        continue
                    ins = emit(e)
                    if inc is not None:
                        ins.then_inc(inc[0], inc[1])
                if name == "sync":
                    for s, v in self.fin:
                        e.wait_ge(s, v)

            @block.sync
            def _(e):
                run(e, "sync")

            @block.scalar
            def _(e):
                run(e, "act")

            @block.vector
            def _(e):
                run(e, "dve")

            @block.gpsimd
            def _(e):
                run(e, "pool")

            @block.tensor
            def _(e):
                run(e, "pe")


def mkcfg(D=4096, NT=4096, HA=12, HB=4, HQ=12, HKV=4, FF=16384, L=2):
    c = dict(D=D, NT=NT, HA=HA, HB=HB, HQ=HQ, HKV=HKV, FF=FF, L=L)
    c["T"] = 512
    c["KC"] = D // 128
    c["segs"] = [("qa", HA * 128), ("ka", HA * 128), ("va", HA * 128), ("qb", 2 * HB * 128), ("kb", 2 * HB * 128),
                 ("vb", HB * 256), ("qc", HQ * 128), ("kc", HKV * 128), ("vc", HKV * 128), ("ga", D), ("gb", D), ("gc", D)]
    c["NIN"] = sum(w for _, w in c["segs"])
    c["NQKH"] = 2 * HA + 4 * HB + HQ + HKV
    c["VW"] = HA * 128 + HB * 256 + HKV * 128
    return c


def windows(cfg):
    rows = cfg["NT"] // 64
    rs_ = rows // 2
    out = []
    ncol = 0
    for r in range(rows):
        sp = min(max(r - 4, 0), rows - 8)
        base = 0 if r < rs_ else rs_
        ss_ = base + min(max(r - base - 4, 0), rs_ - 8)
        lo, hi = min(sp, ss_), max(sp, ss_) + 8
        if (hi - lo) % 2:
            if hi < rows:
                hi += 1
            else:
                lo -= 1
        nt = (hi - lo) // 2
        special = not (sp == ss_ and hi - lo == 8)
        dr0 = lo - r + 7
        out.append(dict(r=r, lo=lo, nt=nt, special=special, dr0=dr0, wp=(sp, sp + 8), ws=(ss_, ss_ + 8), col=ncol))
        for t in range(nt):
            for kr in (lo + 2 * t, lo + 2 * t + 1):
                valid = (sp <= kr < sp + 8) or (ss_ <= kr < ss_ + 8)
                if valid:
                    assert 0 <= dr0 + 2 * t <= 13 and -7 <= kr - r <= 7, (r, kr)
        assert nt * 64 <= 512
        if special:
            ncol += nt
    return out, max(ncol, 1)


def build(cfg):
    D, NT, HA, HB, HQ, HKV, FF, L = (cfg[k] for k in ("D", "NT", "HA", "HB", "HQ", "HKV", "FF", "L"))
    T, KC, NIN, NQKH, VW = cfg["T"], cfg["KC"], cfg["NIN"], cfg["NQKH"], cfg["VW"]
    NTT, NS, NKT, NQB = NT // T, T // 128, NT // 128, NT // 512
    G = HQ // HKV
    NSUB = NT // 128
    wins, NRM = windows(cfg)
    rows = NT // 64
    KSH = (FF // 2) // D
    NUPB = (FF // 2) // 512
    assert KSH * D * 2 == FF

    nc = bass.Bass("TRN2", target_bir_lowering=False)

    def inp(name, shape, dt=F32):
        return nc.dram_tensor(name, list(shape), dt, kind="ExternalInput").ap()

    def scr(name, shape, dt):
        return nc.dram_tensor(name, list(shape), dt, kind="Internal").ap()

    x_in = inp("x", [NT, D])
    w_in = inp("w_in", [L, D, NIN])
    w_oa = inp("w_oa", [L, HA * 128, D])
    w_ob = inp("w_ob", [L, HB * 256, D])
    w_oc = inp("w_oc", [L, HQ * 128, D])
    w_out = inp("w_out", [L, D, D])
    w_up = inp("w_up", [L, D, FF])
    w_down = inp("w_down", [L, FF, D])
    g1 = inp("g1", [L, 128, D])
    g2 = inp("g2", [L, 128, D])
    gains_in = inp("gains", [L, 128, 6 * 128])
    lamv_in = inp("lamv", [L, 128, 4 * 128])
    subln_in = inp("subln", [L, 128, 2])
    tab_in = inp("tab", [L, 128, HA * 14 * 64])
    colmask_in = inp("colmask", [128, 64])
    rowmask_in = inp("rowmask", [128, NRM])
    seqmask_in = inp("seqmask", [128, NQB * NKT])
    rope_in = inp("rope", [NT, 4 * 128])
    ident_in = inp("ident", [128, 128], BF16)
    ones_in = inp("ones", [128, 128], BF16)
    y_out = nc.dram_tensor("y", [NT, D], F32, kind="ExternalOutput").ap()

    wb_in = [scr("wb_in%d" % l, [NIN // 512, 128, D // 128, 512], BF16) for l in range(L)]
    wb_o = [scr("wb_o%d" % l, [D // 512, 128, D // 128, 512], BF16) for l in range(L)]
    wb_out = [scr("wb_out%d" % l, [D // 512, 128, D // 128, 512], BF16) for l in range(L)]
    wb_up = [scr("wb_up%d" % l, [FF // 512, 128, D // 128, 512], BF16) for l in range(L)]
    wb_down = [scr("wb_down%d" % l, [D // 512, 128, FF // 128, 512], BF16) for l in range(L)]
    QT = scr("QT", [NQKH, 128, NT], BF16)
    VS = scr("VS", [NT, VW], BF16)
    GS = scr("GS", [3 * D, NT], BF16)
    OT = scr("OT", [D, NT], BF16)
    XB = scr("XB", [NT, D], F32)

    S = Sched()
    st = contextlib.ExitStack()
    with st:
        ARENA_B = 207 * 1024
        arena = st.enter_context(nc.sbuf_tensor("arena", [128, ARENA_B // 2], BF16))
        PS = [st.enter_context(nc.psum_tensor("ps%d" % i, [128, 512], F32)) for i in range(8)]

        class Alloc:
            def __init__(self, lo, hi):
                self.lo, self.hi, self.p = lo, hi, lo

            def get(self, shape, dt):
                n = int(np.prod(shape)) * (4 if dt == F32 else 2)
                n = (n + 31) // 32 * 32
                off = self.p
                self.p += n
                assert self.p <= self.hi, ("SBUF arena overflow", self.p, self.hi)
                v = arena[:, off // 2:(off + n) // 2]
                if dt == F32:
                    v = v.bitcast(F32)
                ne = int(np.prod(shape))
                v = v[:, 0:ne]
                if len(shape) == 2:
                    v = v.rearrange("p (a b) -> p a b", a=shape[0])
                elif len(shape) == 3:
                    v = v.rearrange("p (a b c) -> p a b c", a=shape[0], b=shape[1])
                return v

        perm = Alloc(0, ARENA_B)
        identv = perm.get([128], BF16)
        onesv = perm.get([128], BF16)
        sst = perm.get([L * 2 * NSUB], F32)
        rstd1 = perm.get([NSUB], F32)
        rstd2 = perm.get([NSUB], F32)
        ssp = perm.get([NS * (D // 512)], F32)
        seqm = perm.get([NQB * NKT], F32)
        rowm = perm.get([NRM], F32)
        gainsv = perm.get([6, 128], F32)
        lamv = perm.get([4, 128], F32)
        sublnv = perm.get([2], F32)
        smallv = perm.get([64], F32)
        colmv = perm.get([64], F32)
        PERM_END = perm.p

        def dma(out, in_):
            return lambda e: e.dma_start(out=out, in_=in_)

        def mm(out, lhsT, rhs, a, b):
            return lambda e: e.matmul(out=out, lhsT=lhsT, rhs=rhs, start=a, stop=b)

        def tr(out, in_):
            return lambda e: e.transpose(out=out, in_=in_, identity=identv)

        def act(out, in_, func, bias=None, scale=None, accum=None):
            kw = {}
            if bias is not None:
                kw["bias"] = bias
            if scale is not None:
                kw["scale"] = scale
            if accum is not None:
                kw["accum_out"] = accum
            return lambda e: e.activation(out=out, in_=in_, func=func, **kw)

        def tt(out, a, b, op):
            return lambda e: e.tensor_tensor(out=out, in0=a, in1=b, op=op)

        def stt(out, a, sc, b, op0, op1):
            return lambda e: e.scalar_tensor_tensor(out=out, in0=a, scalar=sc, in1=b, op0=op0, op1=op1)

        def ts(out, a, s1, s2, op0, op1=None):
            if op1 is None:
                return lambda e: e.tensor_scalar(out=out, in0=a, scalar1=s1, scalar2=None, op0=op0)
            return lambda e: e.tensor_scalar(out=out, in0=a, scalar1=s1, scalar2=s2, op0=op0, op1=op1)

        def cp(eng, out, in_):
            if eng == "act":
                return lambda e: e.copy(out=out, in_=in_)
            return lambda e: e.tensor_copy(out=out, in_=in_)

        def tpview(i):
            return PS[i][:].bitcast(BF16)[:, 0:512].rearrange("p (a b) -> p a b", a=4)

        S.add("sync", dma(identv, ident_in), writes=["ident"], dma_key="c0")
        S.add("sync", dma(onesv, ones_in), writes=["ones"], dma_key="c1")
        S.add("sync", dma(seqm, seqmask_in), writes=["seqm"], dma_key="c2")
        S.add("sync", dma(rowm, rowmask_in), writes=["rowm"], dma_key="c3")
        S.add("sync", dma(colmv, colmask_in), writes=["colm"], dma_key="c4")
        S.add("dve", lambda e: e.memset(sst, 0.0), writes=["sst"])

        cvk = [0]

        def convert(dst, src, name, l):
            R = src.shape[0]
            for r0 in range(0, R, 128):
                S.add("pool", dma(dst[:, :, r0 // 128, :].rearrange("n p j -> p n j"),
                                  src[r0:r0 + 128, :].rearrange("p (n j) -> p n j", j=512)), writes=[("wb", name, l, r0 // 128)],
                      dma_key=("cv", cvk[0] % 8), nobar=True)
                cvk[0] += 1

        def convert_o(l):
            r = 0
            for nm, src in (("oa", w_oa[l]), ("ob", w_ob[l]), ("oc", w_oc[l])):
                R = src.shape[0]
                for r0 in range(0, R, 128):
                    S.add("pool", dma(wb_o[l][:, :, (r + r0) // 128, :].rearrange("n p j -> p n j"),
                                      src[r0:r0 + 128, :].rearrange("p (n j) -> p n j", j=512)),
                          writes=[("wb", "o", l, (r + r0) // 128)], dma_key=("cv", cvk[0] % 8), nobar=True)
                    cvk[0] += 1
                r += R

        for l in range(L):
            convert(wb_in[l], w_in[l], "in", l)
            convert_o(l)
            convert(wb_out[l], w_out[l], "out", l)
            convert(wb_up[l], w_up[l], "up", l)
            convert(wb_down[l], w_down[l], "down", l)

        slab_i = [0]
        SLABS = None

        def slab_load(mat, name, l, krow0, col0):
            slot = slab_i[0] % 2
            slab_i[0] += 1
            S.add("sync", dma(SLABS[slot], mat[col0 // 512, :, krow0 // 128:krow0 // 128 + KC, :]),
                  reads=[("wb", name, l, krow0 // 128 + c) for c in range(KC)], writes=[("slab", slot)],
                  dma_key=("slab", slot))
            return slot

        def run_steps(steps):
            slots = [None] * len(steps)
            if steps:
                slots[0] = slab_load(*steps[0][0])
            for j, (spec, fn) in enumerate(steps):
                if j + 1 < len(steps):
                    slots[j + 1] = slab_load(*steps[j + 1][0])
                fn(slots[j])

        rot = {}

        def nxt(name, n):
            v = rot.get(name, 0)
            rot[name] = v + 1
            return v % n

        coltab = []
        qk_base = {"qa": 0, "ka": HA, "qb": 2 * HA, "kb": 2 * HA + 2 * HB, "qc": 2 * HA + 4 * HB, "kc": 2 * HA + 4 * HB + HQ}
        v_base = {"va": 0, "vb": HA * 128, "vc": HA * 128 + HB * 256}
        g_base = {"ga": 0, "gb": D, "gc": 2 * D}
        gain_idx = {"qa": 0, "ka": 1, "qb": 2, "kb": 3, "qc": 4, "kc": 5}
        for nm, w in cfg["segs"]:
            for i in range(w // 512):
                coltab.append((nm, i))
        assert len(coltab) * 512 == NIN

        for l in range(L):
            lam_init = 0.8 - 0.6 * math.exp(-0.3 * l)
            x_src = x_in if l == 0 else XB
            x_dst_final = y_out if l == L - 1 else XB

            S.barrier()
            A = Alloc(PERM_END, ARENA_B)
            SLABS = [A.get([KC, 512], BF16) for _ in range(2)]
            hT = A.get([KC, 512], BF16)
            gbc = A.get([D], F32)
            xt = [A.get([D], F32) for _ in range(2)]
            hb = [A.get([D], BF16) for _ in range(2)]
            ropev = A.get([NS, 4 * 128], F32)
            stg = [A.get([4, 512], BF16) for _ in range(3)]
            tmpf = [A.get([512], F32) for _ in range(7)]
            xbb = [A.get([512], BF16) for _ in range(2)]
            S.add("sync", dma(gbc, g1[l]), writes=["gbc"], dma_key="gbc")
            S.add("sync", dma(gainsv.rearrange("p a b -> p (a b)"), gains_in[l]), writes=["gains"], dma_key="c0")

            def tmp():
                i = nxt("tmp", 7)
                return tmpf[i], ("tmp", i)

            def a1(ttile, l=l):
                S.add("sync", dma(ropev.rearrange("p s k -> p s k"), rope_in[ttile * T:(ttile + 1) * T, :].rearrange("(s p) k -> p s k", p=128)),
                      writes=["rope"], dma_key="rope")
                for s in range(NS):
                    b = s % 2
                    tok0 = ttile * T + s * 128
                    sub = ttile * NS + s
                    col = l * 2 * NSUB + sub
                    S.add("sync", dma(xt[b], x_src[tok0:tok0 + 128, :]), reads=[("xb", sub, n) for n in range(D // 512)],
                          writes=[("xt", b)], dma_key=("xt", b))
                    S.add("act", act(hb[b], xt[b], AF.Square, accum=sst[:, col:col + 1]), reads=[("xt", b), "sst"],
                          writes=[("hb", b), ("sstc", col)])
                    S.add("act", act(smallv[:, 0:1], sst[:, col:col + 1], AF.Ln, bias=EPS, scale=1.0 / D),
                          reads=[("sstc", col)], writes=["small0"])
                    S.add("act", act(rstd1[:, sub:sub + 1], smallv[:, 0:1], AF.Exp, scale=-0.5), reads=["small0"],
                          writes=[("rstd1", sub)])
                    S.add("dve", stt(hb[b], xt[b], rstd1[:, sub:sub + 1], gbc, ALU.mult, ALU.mult),
                          reads=[("xt", b), ("rstd1", sub), "gbc"], writes=[("hb", b)])
                    for c4 in range(KC // 4):
                        tb = 4 + nxt("tp", 2)
                        tv = tpview(tb)
                        for j in range(4):
                            c = c4 * 4 + j
                            S.add("pe", tr(tv[:, j, :], hb[b][:, c * 128:(c + 1) * 128]), reads=[("hb", b), "ident"],
                                  writes=[("ps", tb)])
                        eng = "act" if c4 % 2 else "dve"
                        S.add(eng, cp(eng, hT[:, c4 * 4:(c4 + 1) * 4, s * 128:(s + 1) * 128], tv), reads=[("ps", tb)],
                              writes=["hT"])

            def a2_step(ttile, nb, slot, l=l):
                nm, bi = coltab[nb]
                slab = SLABS[slot]
                kind = nm[0]
                if kind == "g":
                    si = nxt("stg", 3)
                    sg = stg[si]
                    for f in range(4):
                        a = nxt("acc", 4)
                        for kc in range(KC):
                            S.add("pe", mm(PS[a][:], slab[:, kc, f * 128:(f + 1) * 128], hT[:, kc, :], kc == 0, kc == KC - 1),
                                  reads=["hT", ("slab", slot)], writes=[("ps", a)])
                        S.add("act", act(sg[:, f, :], PS[a][:], AF.Sigmoid), reads=[("ps", a)], writes=[("stg", si)])
                    r0 = g_base[nm] + bi * 512
                    S.add("act", dma(GS[r0:r0 + 512, ttile * T:(ttile + 1) * T].rearrange("(f p) t -> p f t", p=128), sg),
                          reads=[("stg", si)], dma_key=("stg", si))
                    return
                si = nxt("stg", 3)
                sg = stg[si]
                for s in range(NS):
                    a = nxt("acc", 4)
                    for kc in range(KC):
                        S.add("pe", mm(PS[a][:], hT[:, kc, s * 128:(s + 1) * 128], slab[:, kc, :], kc == 0, kc == KC - 1),
                              reads=["hT", ("slab", slot)], writes=[("ps", a)])
                    if kind == "v":
                        S.add("act", cp("act", sg[:, s, :], PS[a][:]), reads=[("ps", a)], writes=[("stg", si)])
                        continue
                    xq, rq = tmp()
                    S.add("act", cp("act", xq, PS[a][:]), reads=[("ps", a)], writes=[rq])
                    sq, rsq = tmp()
                    S.add("dve", tt(sq, xq, xq, ALU.mult), reads=[rq], writes=[rsq])
                    S.add("dve", lambda e, sq=sq: e.tensor_reduce(out=smallv[:, 4:8], in_=sq.rearrange("p (h d) -> p h d", h=4),
                                                                 axis=AX.X, op=ALU.add), reads=[rsq], writes=["small1"])
                    S.add("act", act(smallv[:, 8:12], smallv[:, 4:8], AF.Ln, bias=EPS, scale=1.0 / 128), reads=["small1"],
                          writes=["small2"])
                    S.add("act", act(smallv[:, 12:16], smallv[:, 8:12], AF.Exp, scale=-0.5), reads=["small2"], writes=["small3"])
                    xq3 = xq.rearrange("p (h d) -> p h d", h=4)
                    S.add("dve", tt(xq3, xq3, smallv[:, 12:16].unsqueeze(2).to_broadcast([128, 4, 128]), ALU.mult),
                          reads=[rq, "small3"], writes=[rq])
                    gi = gain_idx[nm]
                    gb_ = gainsv[:, gi, :].unsqueeze(1).to_broadcast([128, 4, 128])
                    bi_ = nxt("xbb", 2)
                    xb_ = xbb[bi_]
                    xb3 = xb_.rearrange("p (h d) -> p h d", h=4)
                    if nm[1] == "a":
                        S.add("dve", tt(xb3, xq3, gb_, ALU.mult), reads=[rq, "gains"], writes=[("xbb", bi_)])
                    else:
                        k0 = 0 if nm[1] == "b" else 2
                        GG, dd = (1, 64) if nm[1] == "b" else (2, 32)
                        S.add("dve", tt(xq3, xq3, gb_, ALU.mult), reads=[rq, "gains"], writes=[rq])
                        xr, rxr = tmp()
                        xr3 = xr.rearrange("p (h d) -> p h d", h=4)
                        cosb = ropev[:, s, k0 * 128:(k0 + 1) * 128].unsqueeze(1).to_broadcast([128, 4, 128])
                        S.add("dve", tt(xr3, xq3, cosb, ALU.mult), reads=[rq, "rope"], writes=[rxr])
                        t2, rt2 = tmp()
                        sinv = ropev[:, s, (k0 + 1) * 128:(k0 + 2) * 128].rearrange("p (g two d) -> p g two d", g=GG, two=2)
                        x5 = xq.rearrange("p (h g two d) -> p h g two d", h=4, g=GG, two=2)
                        t5 = t2.rearrange("p (h g two d) -> p h g two d", h=4, g=GG, two=2)
                        for o_, i_ in ((0, 1), (1, 0)):
                            sb_ = sinv[:, :, o_, :].unsqueeze(1).to_broadcast([128, 4, GG, dd])
                            S.add("dve", tt(t5[:, :, :, o_, :], x5[:, :, :, i_, :], sb_, ALU.mult), reads=[rq, "rope"],
                                  writes=[rt2])
                        S.add("dve", tt(xb_, xr, t2, ALU.add), reads=[rxr, rt2], writes=[("xbb", bi_)])
                    tb = 4 + nxt("tp", 2)
                    tv = tpview(tb)
                    for h in range(4):
                        S.add("pe", tr(tv[:, h, :], xb_[:, h * 128:(h + 1) * 128]), reads=[("xbb", bi_), "ident"],
                              writes=[("ps", tb)])
                    eng = "act" if s % 2 else "dve"
                    S.add(eng, cp(eng, sg[:, :, s * 128:(s + 1) * 128], tv), reads=[("ps", tb)], writes=[("stg", si)])
                if kind == "v":
                    c0 = v_base[nm] + bi * 512
                    S.add("act", dma(VS[ttile * T:(ttile + 1) * T, c0:c0 + 512].rearrange("(s p) c -> p s c", p=128), sg),
                          reads=[("stg", si)], dma_key=("stg", si))
                else:
                    h0 = qk_base[nm] + bi * 4
                    S.add("act", dma(QT[h0:h0 + 4, :, ttile * T:(ttile + 1) * T].rearrange("h d t -> d h t"), sg),
                          reads=[("stg", si)], dma_key=("stg", si))

            steps = []
            for ttile in range(NTT):
                for nb in range(NIN // 512):
                    def fn(slot, ttile=ttile, nb=nb):
                        if nb == 0:
                            a1(ttile)
                        a2_step(ttile, nb, slot)
                    steps.append(((wb_in[l], "in", l, 0, nb * 512), fn))
            run_steps(steps)

            S.barrier()
            Bm = Alloc(PERM_END, ARENA_B)
            tabv = Bm.get([HA * 14, 64], F32)
            kTb = [Bm.get([NT], BF16) for _ in range(3)]
            qTb = [Bm.get([NT], BF16) for _ in range(3)]
            vEb = [Bm.get([NKT, 256], BF16) for _ in range(2)]
            vOb = [Bm.get([NKT, 128], BF16) for _ in range(2)]
            oab = [Bm.get([NT], BF16) for _ in range(2)]
            ptl = [Bm.get([512], BF16) for _ in range(6)]
            tmpb = [Bm.get([512], F32) for _ in range(12)]
            ostg = [Bm.get([2, 512], BF16) for _ in range(3)]
            scale = 128 ** -0.5

            def tmpB():
                i = nxt("tmpb", 12)
                return tmpb[i], ("tmpb", i)

            def ptile():
                i = nxt("pt", 6)
                return ptl[i], ("pt", i)

            S.add("sync", dma(tabv.rearrange("p a b -> p (a b)"), tab_in[l]), writes=["tab"], dma_key="tab")
            S.add("dve", tt(tabv, tabv, colmv.unsqueeze(1).to_broadcast([128, HA * 14, 64]), ALU.add), reads=["tab", "colm"],
                  writes=["tab"])
            S.add("sync", dma(lamv.rearrange("p a b -> p (a b)"), lamv_in[l]), writes=["lamv"], dma_key="c1")
            S.add("sync", dma(sublnv, subln_in[l]), writes=["subln"], dma_key="c2")
            t0_, r0_ = tmpB()
            S.add("dve", tt(t0_[:, 0:128], lamv[:, 0, :], lamv[:, 1, :], ALU.mult), reads=["lamv"], writes=[r0_])
            S.add("dve", tt(t0_[:, 128:256], lamv[:, 2, :], lamv[:, 3, :], ALU.mult), reads=["lamv"], writes=[r0_])
            S.add("dve", lambda e, t0_=t0_: e.tensor_reduce(out=smallv[:, 20:22], in_=t0_[:, 0:256].rearrange("p (a d) -> p a d", a=2),
                                                           axis=AX.X, op=ALU.add), reads=[r0_], writes=["small5"])
            S.add("act", act(smallv[:, 22:24], smallv[:, 20:22], AF.Exp), reads=["small5"], writes=["small6"])
            S.add("dve", tt(smallv[:, 16:17], smallv[:, 23:24], smallv[:, 22:23], ALU.subtract), reads=["small6"], writes=["neglam"])
            S.add("dve", ts(smallv[:, 16:17], smallv[:, 16:17], -lam_init, None, ALU.add), reads=["neglam"], writes=["neglam"])

            ldk = [0]

            def load_head(buf_list, idx, src_ap, nm):
                i = nxt(nm, len(buf_list))
                S.add("sync", dma(buf_list[i] if src_ap is None else buf_list[i], src_ap), writes=[(nm, i)], dma_key=(nm, i))
                return buf_list[i], (nm, i)

            for h in range(HA):
                kT, rk = load_head(kTb, 0, QT[qk_base["ka"] + h], "kT")
                qT, rqh = load_head(qTb, 0, QT[qk_base["qa"] + h], "qT")
                iv = nxt("vE", 2)
                vE = vEb[iv][:, :, 0:128]
                S.add("sync", dma(vE, VS[:, h * 128:(h + 1) * 128].rearrange("(k p) d -> p k d", p=128)), writes=[("vE", iv)],
                      dma_key=("vE", iv))
                io = nxt("vO", 2)
                vO = vOb[io]
                S.add("sync", dma(vO[:, 0:NKT - 1, :], VS[64:64 + (NKT - 1) * 128, h * 128:(h + 1) * 128].rearrange("(k p) d -> p k d", p=128)),
                      writes=[("vO", io)], dma_key=("vO", io))
                ioa = nxt("oa", 2)
                oa = oab[ioa]

                def stage1(w, kT=kT, qT=qT, rk=rk, rqh=rqh, h=h):
                    r, lo, nt = w["r"], w["lo"], w["nt"]
                    a = nxt("sps", 3)
                    for t in range(nt):
                        S.add("pe", mm(PS[a][:, t * 64:(t + 1) * 64], kT[:, (lo + 2 * t) * 64:(lo + 2 * t) * 64 + 128],
                                       qT[:, r * 64:(r + 1) * 64], True, True), reads=[rk, rqh], writes=[("ps", a)])
                    sb_, rsb = tmpB()
                    d0 = w["dr0"]
                    tabs = tabv.rearrange("p (h r) c -> p h r c", h=HA)[:, h, d0:d0 + 2 * (nt - 1) + 1:2, :]
                    S.add("dve", stt(sb_[:, 0:nt * 64].rearrange("p (t c) -> p t c", t=nt),
                                     PS[a][:, 0:nt * 64].rearrange("p (t c) -> p t c", t=nt), scale, tabs, ALU.mult, ALU.add),
                          reads=[("ps", a), "tab"], writes=[rsb])
                    p_, rp = ptile()
                    if not w["special"]:
                        S.add("act", act(p_[:, 0:nt * 64], sb_[:, 0:nt * 64], AF.Exp), reads=[rsb], writes=[rp])
                    else:
                        for t in range(nt):
                            c = w["col"] + t
                            S.add("act", act(p_[:, t * 64:(t + 1) * 64], sb_[:, t * 64:(t + 1) * 64], AF.Exp, bias=rowm[:, c:c + 1]),
                                  reads=[rsb, "rowm"], writes=[rp])
                    return p_, rp

                def stage2(w, p_, rp, vE=vE, vO=vO, iv=iv, io=io, oa=oa, ioa=ioa):
                    r, lo, nt = w["r"], w["lo"], w["nt"]
                    if lo % 2 == 0:
                        vt, rv, base = vE, ("vE", iv), lo // 2
                    else:
                        vt, rv, base = vO, ("vO", io), (lo - 1) // 2
                    po = 3 + nxt("aps", 2)
                    pd = 5 + nxt("apd", 2)
                    for t in range(nt):
                        S.add("pe", mm(PS[po][:, 0:64], vt[:, base + t, :], p_[:, t * 64:(t + 1) * 64], t == 0, t == nt - 1),
                              reads=[rv, rp], writes=[("ps", po)])
                    for t in range(nt):
                        S.add("pe", mm(PS[pd][:, 0:64], onesv, p_[:, t * 64:(t + 1) * 64], t == 0, t == nt - 1),
                              reads=["ones", rp], writes=[("ps", pd)])
                    rd, rrd = tmpB()
                    S.add("dve", lambda e, rd=rd, pd=pd: e.reciprocal(out=rd[:, 0:64], in_=PS[pd][:, 0:64]), reads=[("ps", pd)],
                          writes=[rrd])
                    S.add("dve", tt(oa[:, r * 64:(r + 1) * 64], PS[po][:, 0:64], rd[:, 0:64], ALU.mult), reads=[("ps", po), rrd],
                          writes=[("oa", ioa)])

                pend = stage1(wins[0])
                for ri in range(rows):
                    nx = stage1(wins[ri + 1]) if ri + 1 < rows else None
                    stage2(wins[ri], *pend)
                    pend = nx
                S.add("act", dma(OT[h * 128:(h + 1) * 128, :], oa), reads=[("oa", ioa)], dma_key=("oa", ioa))

            def attn_unit(kT, rk, vt, rv, nh, qT, rqh, qb, evac):
                def pv(kt, p_, rp):
                    for e_ in range(nh):
                        S.add("pe", mm(PS[3 + e_][:], vt[:, kt, e_ * 128:(e_ + 1) * 128], p_, kt == 0, kt == NKT - 1),
                              reads=[rv, rp], writes=[("ps", 3 + e_)])
                    S.add("pe", mm(PS[5][:], onesv, p_, kt == 0, kt == NKT - 1), reads=["ones", rp], writes=[("ps", 5)])

                pend = None
                for kt in range(NKT):
                    a = nxt("sps", 3)
                    S.add("pe", mm(PS[a][:], kT[:, kt * 128:(kt + 1) * 128], qT[:, qb * 512:(qb + 1) * 512], True, True),
                          reads=[rk, rqh], writes=[("ps", a)])
                    p_, rp = ptile()
                    c = qb * NKT + kt
                    S.add("act", act(p_, PS[a][:], AF.Exp, bias=seqm[:, c:c + 1], scale=scale), reads=[("ps", a), "seqm"],
                          writes=[rp])
                    if pend is not None:
                        pv(*pend)
                    pend = (kt, p_, rp)
                pv(*pend)
                rd, rrd = tmpB()
                S.add("dve", lambda e, rd=rd: e.reciprocal(out=rd, in_=PS[5][:]), reads=[("ps", 5)], writes=[rrd])
                evac(rd, rrd)

            ob_row0 = HA * 128
            for hb_ in range(HB):
                iv = nxt("vE", 2)
                c0 = v_base["vb"] + hb_ * 256
                S.add("sync", dma(vEb[iv], VS[:, c0:c0 + 256].rearrange("(k p) d -> p k d", p=128)), writes=[("vE", iv)],
                      dma_key=("vE", iv))
                ks, qs = [], []
                for pr in range(2):
                    ks.append(load_head(kTb, 0, QT[qk_base["kb"] + hb_ * 2 + pr], "kT"))
                    qs.append(load_head(qTb, 0, QT[qk_base["qb"] + hb_ * 2 + pr], "qT"))
                for qb in range(NQB):
                    on = []
                    for pr in range(2):
                        dst = [tmpB(), tmpB()]

                        def evac(rd, rrd, dst=dst):
                            for e_ in range(2):
                                S.add("dve", tt(dst[e_][0], PS[3 + e_][:], rd, ALU.mult), reads=[("ps", 3 + e_), rrd],
                                      writes=[dst[e_][1]])
                        attn_unit(ks[pr][0], ks[pr][1], vEb[iv], ("vE", iv), 2, qs[pr][0], qs[pr][1], qb, evac)
                        on.append(dst)
                    for e_ in range(2):
                        S.add("dve", stt(on[0][e_][0], on[1][e_][0], smallv[:, 16:17], on[0][e_][0], ALU.mult, ALU.add),
                              reads=[on[1][e_][1], on[0][e_][1], "neglam"], writes=[on[0][e_][1]])
                        sq, rsq = ptile()
                        S.add("dve", tt(sq, on[0][e_][0], on[0][e_][0], ALU.mult), reads=[on[0][e_][1]], writes=[rsq])
                        S.add("pe", mm(PS[6][:], onesv, sq, e_ == 0, e_ == 1), reads=["ones", rsq], writes=[("ps", 6)])
                    tl, rtl = tmpB()
                    S.add("act", act(tl, PS[6][:], AF.Ln, bias=EPS, scale=1.0 / 256), reads=[("ps", 6)], writes=[rtl])
                    S.add("act", act(tl, tl, AF.Exp, bias=math.log(1.0 - lam_init), scale=-0.5), reads=[rtl], writes=[rtl])
                    io_ = nxt("ostg", 3)
                    for e_ in range(2):
                        S.add("dve", stt(ostg[io_][:, e_, :], on[0][e_][0], sublnv[:, e_:e_ + 1], tl, ALU.mult, ALU.mult),
                              reads=[on[0][e_][1], rtl, "subln"], writes=[("ostg", io_)])
                    r0 = ob_row0 + hb_ * 256
                    S.add("act", dma(OT[r0:r0 + 256, qb * 512:(qb + 1) * 512].rearrange("(e p) t -> p e t", p=128), ostg[io_]),
                          reads=[("ostg", io_)], dma_key=("ostg", io_))

            oc_row0 = HA * 128 + HB * 256
            for n in range(HKV):
                iv = nxt("vE", 2)
                c0 = v_base["vc"] + n * 128
                vC = vEb[iv][:, :, 0:128]
                S.add("sync", dma(vC, VS[:, c0:c0 + 128].rearrange("(k p) d -> p k d", p=128)), writes=[("vE", iv)],
                      dma_key=("vE", iv))
                kT, rk = load_head(kTb, 0, QT[qk_base["kc"] + n], "kT")
                for g_ in range(G):
                    hq = n * G + g_
                    qT, rqh = load_head(qTb, 0, QT[qk_base["qc"] + hq], "qT")
                    for qb in range(NQB):
                        io_ = nxt("ostg", 3)

                        def evac(rd, rrd, io_=io_):
                            S.add("dve", tt(ostg[io_][:, 0, :], PS[3][:], rd, ALU.mult), reads=[("ps", 3), rrd],
                                  writes=[("ostg", io_)])
                        attn_unit(kT, rk, vC, ("vE", iv), 1, qT, rqh, qb, evac)
                        r0 = oc_row0 + hq * 128
                        S.add("act", dma(OT[r0:r0 + 128, qb * 512:(qb + 1) * 512], ostg[io_][:, 0, :]), reads=[("ostg", io_)],
                              dma_key=("ostg", io_))

            S.barrier()
            C = Alloc(PERM_END, ARENA_B)
            SLABS = [C.get([KC, 512], BF16) for _ in range(2)]
            bufA = C.get([KC, 512], BF16)
            gbc = C.get([D], F32)
            bufB = C.get([KSH * KC, 512], BF16)
            gtl = [C.get([3, 512], BF16) for _ in range(2)]
            tmpc = [C.get([512], F32) for _ in range(6)]
            hb2 = [C.get([512], BF16) for _ in range(2)]
            mT = bufB[:, 0:KC, :]
            uT = bufB
            S.add("sync", dma(gbc, g2[l]), writes=["gbc"], dma_key="gbc")

            def tmpC():
                i = nxt("tmpc", 6)
                return tmpc[i], ("tmpc", i)

            na, nb_, ncc = HA, 2 * HB, HQ
            assert na + nb_ + ncc == KC

            steps = []
            for ttile in range(NTT):
                tsl = slice(ttile * T, (ttile + 1) * T)

                for nb in range(D // 512):
                    def fn(slot, nb=nb, ttile=ttile, tsl=tsl):
                        slab = SLABS[slot]
                        if nb == 0:
                            S.add("sync", dma(bufA, OT[:, tsl].rearrange("(c p) t -> p c t", p=128)), writes=["bufA"], dma_key="bufA")
                        for f in range(4):
                            ft = nb * 4 + f
                            ig = nxt("gt", 2)
                            S.add("sync", dma(gtl[ig], GS[:, tsl].rearrange("(g q) t -> q g t", g=3)[ft * 128:(ft + 1) * 128]),
                                  writes=[("gt", ig)], dma_key=("gt", ig))
                            base = 3 * nxt("oset", 2)
                            for bi, (k0, k1) in enumerate(((0, na), (na, na + nb_), (na + nb_, KC))):
                                for kc in range(k0, k1):
                                    S.add("pe", mm(PS[base + bi][:], slab[:, kc, f * 128:(f + 1) * 128], bufA[:, kc, :], kc == k0,
                                                   kc == k1 - 1), reads=["bufA", ("slab", slot)], writes=[("ps", base + bi)])
                            ms = [tmpC() for _ in range(3)]
                            for bi in range(3):
                                S.add("dve", tt(ms[bi][0], PS[base + bi][:], gtl[ig][:, bi, :], ALU.mult),
                                      reads=[("ps", base + bi), ("gt", ig)], writes=[ms[bi][1]])
                            S.add("dve", tt(ms[0][0], ms[0][0], ms[1][0], ALU.add), reads=[ms[0][1], ms[1][1]], writes=[ms[0][1]])
                            S.add("dve", tt(mT[:, ft, :], ms[0][0], ms[2][0], ALU.add), reads=[ms[0][1], ms[2][1]], writes=["bufB"])
                    steps.append(((wb_o[l], "o", l, 0, nb * 512), fn))

                for nb in range(D // 512):
                    def fn(slot, nb=nb, ttile=ttile, l=l):
                        slab = SLABS[slot]
                        if nb == 0:
                            S.add("dve", lambda e: e.memset(ssp, 0.0),
                                  writes=[("ssp", s_, n_) for s_ in range(NS) for n_ in range(D // 512)])
                        for s in range(NS):
                            sub = ttile * NS + s
                            tok0 = sub * 128
                            a = nxt("acc", 4)
                            for kc in range(KC):
                                S.add("pe", mm(PS[a][:], mT[:, kc, s * 128:(s + 1) * 128], slab[:, kc, :], kc == 0, kc == KC - 1),
                                      reads=["bufB", ("slab", slot)], writes=[("ps", a)])
                            xin, rxi = tmpC()
                            S.add("sync", dma(xin, x_src[tok0:tok0 + 128, nb * 512:(nb + 1) * 512]), reads=[("xb", sub, nb)],
                                  writes=[rxi], dma_key=rxi)
                            S.add("dve", tt(xin, PS[a][:], xin, ALU.add), reads=[("ps", a), rxi], writes=[rxi])
                            S.add("act", dma(XB[tok0:tok0 + 128, nb * 512:(nb + 1) * 512], xin), reads=[rxi], writes=[("xb", sub, nb)],
                                  dma_key=rxi)
                            col = l * 2 * NSUB + NSUB + sub
                            ih = nxt("hb2", 2)
                            S.add("act", act(hb2[ih], xin, AF.Square, accum=ssp[:, s * (D // 512) + nb:s * (D // 512) + nb + 1]),
                                  reads=[rxi], writes=[("hb2", ih), ("ssp", s, nb)])
                            S.add("dve", tt(hb2[ih], xin, gbc[:, nb * 512:(nb + 1) * 512], ALU.mult), reads=[rxi, "gbc"],
                                  writes=[("hb2", ih)])
                            tb = 4 + nxt("tp", 2)
                            tv = tpview(tb)
                            for j in range(4):
                                S.add("pe", tr(tv[:, j, :], hb2[ih][:, j * 128:(j + 1) * 128]), reads=[("hb2", ih), "ident"],
                                      writes=[("ps", tb)])
                            eng = "act" if s % 2 else "dve"
                            S.add(eng, cp(eng, bufA[:, nb * 4:(nb + 1) * 4, s * 128:(s + 1) * 128], tv), reads=[("ps", tb)],
                                  writes=["bufA"])
                        if nb == D // 512 - 1:
                            S.add("dve", lambda e: e.tensor_reduce(out=smallv[:, 32:32 + NS], in_=ssp.rearrange("p (s n) -> p s n", s=NS),
                                                                   axis=AX.X, op=ALU.add),
                                  reads=[("ssp", s_, n_) for s_ in range(NS) for n_ in range(D // 512)], writes=["small8"])
                            S.add("dve", ts(smallv[:, 32:32 + NS], smallv[:, 32:32 + NS], 1.0 / D, EPS, ALU.mult, ALU.add),
                                  reads=["small8"], writes=["small8"])
                            S.add("dve", lambda e, ttile=ttile: e.reciprocal(out=rstd2[:, ttile * NS:(ttile + 1) * NS],
                                                                            in_=smallv[:, 32:32 + NS]), reads=["small8"],
                                  writes=[("rstd2", ttile)])
                    steps.append(((wb_out[l], "out", l, 0, nb * 512), fn))

                for half in range(2):
                    for nb in range(NUPB):
                        def fn(slot, nb=nb):
                            slab = SLABS[slot]
                            for f in range(4):
                                a = nxt("acc", 4)
                                for kc in range(KC):
                                    S.add("pe", mm(PS[a][:], slab[:, kc, f * 128:(f + 1) * 128], bufA[:, kc, :], kc == 0, kc == KC - 1),
                                          reads=["bufA", ("slab", slot)], writes=[("ps", a)])
                                r_, rr = tmpC()
                                S.add("act", act(r_, PS[a][:], AF.Relu), reads=[("ps", a)], writes=[rr])
                                S.add("dve", tt(uT[:, nb * 4 + f, :], r_, r_, ALU.mult), reads=[rr], writes=["bufB"])
                        steps.append(((wb_up[l], "up", l, 0, half * (FF // 2) + nb * 512), fn))
                    for nb in range(D // 512):
                        for ks in range(KSH):
                            def fn(slot, nb=nb, ks=ks, half=half, ttile=ttile):
                                slab = SLABS[slot]
                                if ks == 0:
                                    rot["dset_cur"] = 4 * nxt("dset", 2)
                                base = rot["dset_cur"]
                                for s in range(NS):
                                    for kc in range(KC):
                                        S.add("pe", mm(PS[base + s][:], uT[:, ks * KC + kc, s * 128:(s + 1) * 128], slab[:, kc, :],
                                                       ks == 0 and kc == 0, ks == KSH - 1 and kc == KC - 1),
                                              reads=["bufB", ("slab", slot)], writes=[("ps", base + s)])
                                if ks == KSH - 1:
                                    for s in range(NS):
                                        sub = ttile * NS + s
                                        tok0 = sub * 128
                                        yin, ryi = tmpC()
                                        S.add("sync", dma(yin, XB[tok0:tok0 + 128, nb * 512:(nb + 1) * 512]), reads=[("xb", sub, nb)],
                                              writes=[ryi], dma_key=ryi)
                                        S.add("dve", stt(yin, PS[base + s][:], rstd2[:, sub:sub + 1], yin, ALU.mult, ALU.add),
                                              reads=[("ps", base + s), ryi, ("rstd2", ttile)], writes=[ryi])
                                        dst = XB if half == 0 else x_dst_final
                                        S.add("act", dma(dst[tok0:tok0 + 128, nb * 512:(nb + 1) * 512], yin), reads=[ryi],
                                              writes=[("xb", sub, nb)], dma_key=ryi)
                            steps.append(((wb_down[l], "down", l, half * (FF // 2) + ks * D, nb * 512), fn))
            run_steps(steps)

        S.barrier()
        S.finalize(nc, st)
        S.emit_all(nc)
    return nc


def host_tables(cfg, is_prompt):
    NT = cfg["NT"]
    NKT, NQB = NT // 128, NT // 512
    wins, NRM = windows(cfg)
    seqlen = NT if is_prompt else NT // 2
    pos = (np.arange(NT) % seqlen).astype(np.float32)
    invB = (10000.0 ** (-np.arange(0, 128, 2, dtype=np.float32) / 128)).astype(np.float32)
    angB = pos[:, None] * invB[None, :]
    cB, sB = np.cos(angB), np.sin(angB)
    inv64 = (10000.0 ** (-np.arange(0, 64, 2, dtype=np.float32) / 64)).astype(np.float32)
    pr = np.floor(pos / 64.0).astype(np.float32)
    pc = (pos - pr * 64.0).astype(np.float32)
    a1, a2 = pr[:, None] * inv64[None, :], pc[:, None] * inv64[None, :]
    rope = np.zeros((NT, 4, 128), np.float32)
    rope[:, 0] = np.concatenate([cB, cB], 1)
    rope[:, 1] = np.concatenate([-sB, sB], 1)
    rope[:, 2] = np.concatenate([np.cos(a1), np.cos(a1), np.cos(a2), np.cos(a2)], 1)
    rope[:, 3] = np.concatenate([-np.sin(a1), np.sin(a1), -np.sin(a2), np.sin(a2)], 1)
    seqmask = np.zeros((128, NQB * NKT), np.float32)
    for qb in range(NQB):
        for kt in range(NKT):
            if (qb * 512) // seqlen != (kt * 128) // seqlen:
                seqmask[:, qb * NKT + kt] = NEG
    rowmask = np.zeros((128, NRM), np.float32)
    for w in wins:
        if not w["special"]:
            continue
        lo_, hi_ = w["wp"] if is_prompt else w["ws"]
        for t in range(w["nt"]):
            for half in range(2):
                kr = w["lo"] + 2 * t + half
                if not (lo_ <= kr < hi_):
                    rowmask[half * 64:(half + 1) * 64, w["col"] + t] = NEG
    return rope.reshape(NT, 512), seqmask, rowmask


def host_common(cfg, inputs):
    L, D, HA = cfg["L"], cfg["D"], cfg["HA"]
    f = lambda a: np.ascontiguousarray(np.asarray(a, dtype=np.float32))
    rep = lambda a: np.ascontiguousarray(np.broadcast_to(f(a)[:, None, :], (L, 128, f(a).shape[-1])))
    m = {}
    for k in ("w_in", "w_oa", "w_ob", "w_oc", "w_out", "w_up", "w_down"):
        m[k] = f(inputs[k])
    m["g1"] = rep(inputs["norm_mix"])
    m["g2"] = rep(inputs["norm_mlp"])
    m["gains"] = np.ascontiguousarray(np.concatenate([rep(inputs[k]) for k in ("qn_a", "kn_a", "qn_b", "kn_b", "qn_c", "kn_c")], 2))
    m["lamv"] = np.ascontiguousarray(np.concatenate([rep(inputs[k]) for k in ("lam_q1", "lam_k1", "lam_q2", "lam_k2")], 2))
    m["subln"] = np.ascontiguousarray(f(inputs["subln_b"]).reshape(L, 2, 128).transpose(0, 2, 1))
    rpb = f(inputs["rpb"])
    p = np.arange(128)
    kc = p % 64
    up = (p >= 64).astype(np.int64)
    qc = np.arange(64)
    dc = np.clip(kc[:, None] - qc[None, :], -15, 15) + 15
    dr = np.arange(14)[None, :] + up[:, None]
    tab = rpb[:, :, dr[:, :, None], dc[:, None, :]]
    m["tab"] = np.ascontiguousarray(tab.transpose(0, 2, 1, 3, 4)).reshape(L, 128, HA * 14 * 64)
    cs = np.clip(qc - 8, 0, 48)
    ok = (kc[:, None] >= cs[None, :]) & (kc[:, None] < cs[None, :] + 16)
    m["colmask"] = np.where(ok, 0.0, NEG).astype(np.float32)
    m["ident"] = np.eye(128, dtype=np.float32).astype(ml_dtypes.bfloat16)
    m["ones"] = np.ones((128, 128), np.float32).astype(ml_dtypes.bfloat16)
    return m


_NC_CACHE = {}


def run(cfg, inputs, n_prompt):
    key = tuple(sorted((k, v) for k, v in cfg.items() if not isinstance(v, list)))
    if key not in _NC_CACHE:
        _NC_CACHE[key] = build(cfg)
    nc = _NC_CACHE[key]
    NT, D = cfg["NT"], cfg["D"]
    common = host_common(cfg, inputs)
    xp = np.asarray(inputs["x_prompt"], np.float32)
    xs = np.asarray(inputs["x_sample"], np.float32)
    tabs = {True: host_tables(cfg, True), False: host_tables(cfg, False)}
    in_maps = []
    ncores = 8
    for c in range(ncores):
        isp = c < n_prompt
        if isp:
            xc = xp[c].reshape(NT, D)
        else:
            j = c - n_prompt
            xc = xs[2 * j:2 * j + 2].reshape(NT, D)
        rope, seqmask, rowmask = tabs[isp]
        mm_ = dict(common)
        mm_.update(x=np.ascontiguousarray(xc), rope=rope, seqmask=seqmask, rowmask=rowmask)
        in_maps.append(mm_)
    res = run_bass_kernel_spmd(nc, in_maps, core_ids=list(range(ncores)))
    ys = [np.asarray(r["y"], np.float32) for r in res.results]
    yp = np.stack([ys[c] for c in range(n_prompt)], 0).reshape(xp.shape)
    ysamp = np.concatenate([ys[c].reshape(2, NT // 2, D) for c in range(n_prompt, ncores)], 0).reshape(xs.shape)
    return yp, ysamp


def kernel(**inputs):
    cfg = mkcfg()
    return run(cfg, inputs, 4)
```
